# Optimizing a Trainium2 kernel written in Bass

```python
import jax, jax.numpy as jnp
from jax import lax
import numpy as np

D_MODEL = 2048
BATCH = 4
SEQ = 2048
DEPTH = 1

HEAD_DIM = 64
ATTN_HEADS = D_MODEL // 128
KV_HEADS = ATTN_HEADS // 4
Q_DIM = ATTN_HEADS * HEAD_DIM
KV_DIM = KV_HEADS * HEAD_DIM
WINDOW = 128
ATTN_BLOCK = 128
ROPE_THETA = 10000.0
D_INNER = D_MODEL
SSM_HEAD_DIM = 64
SSM_HEADS = D_INNER // SSM_HEAD_DIM
SSM_GROUPS = 4
D_STATE = 128
CONV_WIDTH = 4
CHUNK = 128
CONV_DIM = D_INNER + 2 * SSM_GROUPS * D_STATE
FFN_HIDDEN = -(-(8 * D_MODEL) // (3 * 256)) * 256
PLE_DIM = 256
IN_DIM = Q_DIM + 2 * KV_DIM + D_INNER + CONV_DIM + SSM_HEADS + 2 * D_MODEL
NORM_EPS = 1e-6
SSM_NORM_EPS = 1e-5

kernel_name = "hybrid_swa_sink_ssd_gated_block"


def rmsnorm(x, g, eps=NORM_EPS):
    xf = x.astype(jnp.float32)
    y = xf * lax.rsqrt(jnp.mean(xf * xf, axis=-1, keepdims=True) + eps)
    return (y * g.astype(jnp.float32)).astype(x.dtype)


def apply_rope(t, positions):
    half = HEAD_DIM // 2
    inv_freq = ROPE_THETA ** (-jnp.arange(half, dtype=jnp.float32) * 2.0 / HEAD_DIM)
    ang = positions.astype(jnp.float32)[..., None] * inv_freq
    cos, sin = jnp.cos(ang)[:, :, None, :], jnp.sin(ang)[:, :, None, :]
    t1, t2 = t[..., :half], t[..., half:]
    return jnp.concatenate([t1 * cos - t2 * sin, t2 * cos + t1 * sin], axis=-1)


def sliding_window_sink_attention(q, k, v, sinks):
    b, s = q.shape[0], q.shape[1]
    nb = s // ATTN_BLOCK
    grp = ATTN_HEADS // KV_HEADS
    qb = q.reshape(b, nb, ATTN_BLOCK, KV_HEADS, grp, HEAD_DIM)

    def banded(t):
        tb = t.reshape(b, nb, ATTN_BLOCK, KV_HEADS, HEAD_DIM)
        prev = jnp.pad(tb, ((0, 0), (1, 0), (0, 0), (0, 0), (0, 0)))[:, :-1]
        return jnp.concatenate([prev, tb], axis=2)

    kw, vw = banded(k), banded(v)
    scores = jnp.einsum('bnqhgd,bnkhd->bnhgqk', qb, kw) * (HEAD_DIM ** -0.5)
    qi = jnp.arange(ATTN_BLOCK)[:, None] + ATTN_BLOCK
    kj = jnp.arange(2 * ATTN_BLOCK)[None, :]
    dist = qi - kj
    key_pos = (jnp.arange(nb) * ATTN_BLOCK)[:, None, None] - ATTN_BLOCK + kj[None]
    valid = (dist >= 0)[None] & (dist < WINDOW)[None] & (key_pos >= 0)
    scores = jnp.where(valid[None, :, None, None], scores, -jnp.inf)
    sink = sinks.astype(jnp.float32).reshape(KV_HEADS, grp)[None, None, :, :, None]
    m = jnp.maximum(scores.max(axis=-1), sink)
    e = jnp.exp(scores - m[..., None])
    probs = e / (e.sum(axis=-1) + jnp.exp(sink - m))[..., None]
    out = jnp.einsum('bnhgqk,bnkhd->bnqhgd', probs, vw)
    return out.reshape(b, s, Q_DIM)


def causal_depthwise_conv(x, w, bias):
    out = lax.conv_general_dilated(
        x, w[:, None, :], window_strides=(1,), padding=[(CONV_WIDTH - 1, 0)],
        dimension_numbers=('NWC', 'WIO', 'NWC'), feature_group_count=x.shape[-1])
    return out + bias


def ssd_chunked(xh, dt, a_neg, bm, cm):
    b, s = xh.shape[0], xh.shape[1]
    nc = s // CHUNK
    e_per = SSM_HEADS // SSM_GROUPS
    xd = (xh * dt[..., None]).reshape(b, nc, CHUNK, SSM_GROUPS, e_per, SSM_HEAD_DIM)
    a = jnp.transpose((dt * a_neg).reshape(b, nc, CHUNK, SSM_GROUPS, e_per), (0, 1, 3, 4, 2))
    a_cs = jnp.cumsum(a, axis=-1)
    bc = bm.reshape(b, nc, CHUNK, SSM_GROUPS, D_STATE)
    cc = cm.reshape(b, nc, CHUNK, SSM_GROUPS, D_STATE)
    tril = jnp.tril(jnp.ones((CHUNK, CHUNK), dtype=bool))
    diff = a_cs[..., :, None] - a_cs[..., None, :]
    decay = jnp.where(tril, jnp.exp(jnp.where(tril, diff, 0.0)), 0.0)
    cb = jnp.einsum('bclgn,bcsgn->bcgls', cc, bc)
    y_diag = jnp.einsum('bcgels,bcsgep->bclgep', cb[:, :, :, None] * decay, xd)
    decay_states = jnp.exp(a_cs[..., -1:] - a_cs)
    states = jnp.einsum('bclgn,bcgel,bclgep->bcgepn', bc, decay_states, xd)
    chunk_decay = jnp.exp(a_cs[..., -1])

    def step(carry, inp):
        st, dec = inp
        return carry * dec[..., None, None] + st, carry

    init = jnp.zeros((b, SSM_GROUPS, e_per, SSM_HEAD_DIM, D_STATE), jnp.float32)
    _, prev = lax.scan(step, init, (jnp.moveaxis(states, 1, 0), jnp.moveaxis(chunk_decay, 1, 0)))
    prev = jnp.moveaxis(prev, 0, 1)
    y_off = jnp.einsum('bclgn,bcgepn,bcgel->bclgep', cc, prev, jnp.exp(a_cs))
    return (y_diag + y_off).reshape(b, s, SSM_HEADS, SSM_HEAD_DIM)


def _dense(key, shape, fan_in):
    return jax.random.normal(key, shape, jnp.float32) * (fan_in ** -0.5)


def setup_inputs(seed: int = 0) -> dict:
    key = jax.random.key(seed)
    ks = jax.random.split(key, 24)
    L = DEPTH
    ones_noise = lambda k, shape: 1.0 + 0.05 * jax.random.normal(k, shape, jnp.float32)
    dt0 = jnp.exp(jax.random.uniform(ks[5], (L, SSM_HEADS), jnp.float32, np.log(1e-3), np.log(1e-1)))
    return {
        "x": jax.random.normal(ks[0], (BATCH, SEQ, D_MODEL), jnp.float32),
        "p": jax.random.normal(ks[1], (DEPTH, BATCH, SEQ, PLE_DIM), jnp.float32),
        "positions": jnp.tile(jnp.arange(SEQ, dtype=jnp.int32)[None], (BATCH, 1)),
        "g_mix": ones_noise(ks[2], (L, D_MODEL)),
        "w_in": _dense(ks[3], (L, D_MODEL, IN_DIM), D_MODEL),
        "conv_w": _dense(ks[4], (L, CONV_WIDTH, CONV_DIM), CONV_WIDTH),
        "conv_b": 0.02 * jax.random.normal(ks[6], (L, CONV_DIM), jnp.float32),
        "dt_bias": dt0 + jnp.log(-jnp.expm1(-dt0)),
        "a_log": jnp.log(jax.random.uniform(ks[7], (L, SSM_HEADS), jnp.float32, 1.0, 16.0)),
        "d_skip": ones_noise(ks[8], (L, SSM_HEADS)),
        "g_ssd": ones_noise(ks[9], (L, D_INNER)),
        "sinks": 0.5 * jax.random.normal(ks[10], (L, ATTN_HEADS), jnp.float32),
        "w_attn_br": _dense(ks[11], (L, Q_DIM, D_MODEL), Q_DIM),
        "w_ssd_br": _dense(ks[12], (L, D_INNER, D_MODEL), D_INNER),
        "w_o": _dense(ks[13], (L, D_MODEL, D_MODEL), D_MODEL),
        "g_ffn": ones_noise(ks[14], (L, D_MODEL)),
        "w_gate": _dense(ks[15], (L, D_MODEL, FFN_HIDDEN), D_MODEL),
        "w_up": _dense(ks[16], (L, D_MODEL, FFN_HIDDEN), D_MODEL),
        "w_down": _dense(ks[17], (L, FFN_HIDDEN, D_MODEL), FFN_HIDDEN),
        "g_ple": ones_noise(ks[18], (L, D_MODEL)),
        "w_ple_gate": _dense(ks[19], (L, D_MODEL, D_MODEL), D_MODEL),
        "w_ple_proj": _dense(ks[20], (L, PLE_DIM, D_MODEL), PLE_DIM),
        "g_final": ones_noise(ks[21], (D_MODEL,)),
    }


def reference(x, p, positions, g_mix, w_in, conv_w, conv_b, dt_bias, a_log, d_skip, g_ssd,
              sinks, w_attn_br, w_ssd_br, w_o, g_ffn, w_gate, w_up, w_down, g_ple,
              w_ple_gate, w_ple_proj, g_final):
    b, s = x.shape[0], x.shape[1]
    f32 = jnp.float32
    sizes = [Q_DIM, KV_DIM, KV_DIM, D_INNER, CONV_DIM, SSM_HEADS, D_MODEL, D_MODEL]
    offsets = [int(o) for o in np.cumsum(sizes)[:-1]]
    h = x
    for i in range(DEPTH):
        u = rmsnorm(h, g_mix[i])
        proj = u @ w_in[i]
        q, k, v, z, xbc, dt_raw, g_a, g_s = jnp.split(proj, offsets, axis=-1)

        q = apply_rope(q.astype(f32).reshape(b, s, ATTN_HEADS, HEAD_DIM), positions)
        k = apply_rope(k.astype(f32).reshape(b, s, KV_HEADS, HEAD_DIM), positions)
        v = v.astype(f32).reshape(b, s, KV_HEADS, HEAD_DIM)
        attn = sliding_window_sink_attention(q, k, v, sinks[i]).astype(x.dtype)
        out_a = attn @ w_attn_br[i]

        xbc = jax.nn.silu(causal_depthwise_conv(xbc, conv_w[i], conv_b[i])).astype(f32)
        xs, bm, cm = jnp.split(xbc, [D_INNER, D_INNER + SSM_GROUPS * D_STATE], axis=-1)
        xh = xs.reshape(b, s, SSM_HEADS, SSM_HEAD_DIM)
        dt = jax.nn.softplus(dt_raw.astype(f32) + dt_bias[i].astype(f32))
        a_neg = -jnp.exp(a_log[i].astype(f32))
        y = ssd_chunked(xh, dt, a_neg,
                        bm.reshape(b, s, SSM_GROUPS, D_STATE),
                        cm.reshape(b, s, SSM_GROUPS, D_STATE))
        y = (y + d_skip[i].astype(f32)[:, None] * xh).reshape(b, s, D_INNER)
        y = rmsnorm(y * jax.nn.silu(z.astype(f32)), g_ssd[i], SSM_NORM_EPS).astype(x.dtype)
        out_s = y @ w_ssd_br[i]

        merged = jax.nn.sigmoid(g_a) * out_a + jax.nn.sigmoid(g_s) * out_s
        h = h + merged @ w_o[i]

        f = rmsnorm(h, g_ffn[i])
        h = h + (jax.nn.silu(f @ w_gate[i]) * (f @ w_up[i])) @ w_down[i]

        gate = jax.nn.sigmoid(rmsnorm(h, g_ple[i]) @ w_ple_gate[i])
        h = h + gate * (p[i] @ w_ple_proj[i])
    return rmsnorm(h, g_final)
```

```python
import math
import numpy as np
from contextlib import ExitStack
import concourse.bass as bass
import concourse.mybir as mybir
from concourse.bass_utils import run_bass_kernel_spmd

F32 = mybir.dt.float32
BF16 = mybir.dt.bfloat16
I32 = mybir.dt.int32
ALU = mybir.AluOpType
AF = mybir.ActivationFunctionType
AX = mybir.AxisListType

D = 2048
KC = 16
Q_DIM = 1024
FF = 5632
PLE = 256
NHEAD_S = 32
OFF_Q, OFF_K, OFF_V, OFF_Z, OFF_XBC, OFF_DT, OFF_GA, OFF_GS, IN_DIM = 0, 1024, 1280, 1536, 3584, 6656, 6688, 8736, 10784
EPS = 1e-6
SSM_EPS = 1e-5
MASKV = -240000.0

ENGINES = ("pe", "act", "dve", "pool", "sp")


class T:
    __slots__ = ("name", "last_w", "readers", "dsem")

    def __init__(self, name):
        self.name = name
        self.last_w = None
        self.readers = []
        self.dsem = None


class Prog:
    def __init__(self, nc, stack):
        self.nc = nc
        self.stack = stack
        self.ops = {e: [] for e in ENGINES}
        self.sems = {}
        self.semval = {}
        self.waited = {e: {} for e in ENGINES}
        for e in ENGINES:
            self._mksem("E_" + e)
        self.n_dsem = 0
        self.epoch = []
        self.dram = T("dram")

    def _mksem(self, key):
        h = self.stack.enter_context(self.nc.semaphore(key))
        self.sems[key] = h
        self.semval[key] = 0
        return key

    def tile(self, name, dma=False):
        t = T(name)
        t.readers = list(self.epoch)
        if dma:
            t.dsem = self._mksem("D%d" % self.n_dsem)
            self.n_dsem += 1
        return t

    def tiles(self, name, n, dma=False):
        return [self.tile("%s%d" % (name, i), dma) for i in range(n)]

    def _collect_waits(self, eng, reads, writes):
        need = {}
        me = "E_" + eng
        pe = eng == "pe"

        def add(ev, skip_same):
            if ev is None:
                return
            k, v = ev
            if skip_same and k == me:
                return
            if need.get(k, 0) < v:
                need[k] = v

        for t in reads:
            add(t.last_w, pe)
        for t in writes:
            add(t.last_w, pe)
            for r in t.readers:
                add(r, pe)
        out = []
        w = self.waited[eng]
        for k, v in need.items():
            if w.get(k, 0) < v:
                w[k] = v
                out.append((k, v))
        return out

    def op(self, eng, fn, reads=(), writes=(), inc=True):
        waits = self._collect_waits(eng, reads, writes)
        key = "E_" + eng
        if inc:
            self.semval[key] += 1
            ev = (key, self.semval[key])
        else:
            ev = (key, self.semval[key] + 1)
        for t in reads:
            if t is not self.dram:
                t.readers.append(ev)
        for t in writes:
            t.last_w = ev
            t.readers = []
        self.ops[eng].append((waits, fn, key if inc else None))
        return ev

    def I(self, eng, name, reads, writes, inc=True, **kw):
        return self.op(eng, lambda e, name=name, kw=kw: getattr(e, name)(**kw), reads, writes, inc)

    def dma(self, eng, out_t, in_t, out_ap, in_ap, sem_t=None):
        sem_t = sem_t or (out_t if out_t.dsem else in_t)
        waits = self._collect_waits(eng, [in_t], [out_t])
        k = sem_t.dsem
        self.semval[k] += 16
        ev = (k, self.semval[k])
        if in_t is not self.dram:
            in_t.readers.append(ev)
        if out_t is not self.dram:
            out_t.last_w = ev
            out_t.readers = []

        def fn(e, out_ap=out_ap, in_ap=in_ap):
            return e.dma_start(out=out_ap, in_=in_ap)
        self.ops[eng].append((waits, fn, ("DMA", k)))
        return ev

    def wait_on(self, eng, ev):
        k, v = ev
        if self.waited[eng].get(k, 0) < v:
            self.waited[eng][k] = v
            self.ops[eng].append(([(k, v)], None, None))

    def barrier(self, hard=False):
        snap = dict(self.semval)
        self.epoch = [(k, v) for k, v in snap.items() if v > 0]
        if hard:
            for e in ENGINES:
                for k, v in snap.items():
                    if v > 0 and k != "E_" + e:
                        self.wait_on(e, (k, v))

    def emit(self):
        nc = self.nc
        with nc.Block() as block:
            def run(engname, e):
                for waits, fn, inc in self.ops[engname]:
                    for k, v in waits:
                        e.wait_ge(self.sems[k], v)
                    if fn is None:
                        continue
                    ins = fn(e)
                    if inc is None:
                        continue
                    if isinstance(inc, tuple):
                        ins.then_inc(self.sems[inc[1]], 16)
                    else:
                        ins.then_inc(self.sems[inc], 1)

            @block.tensor
            def _(e):
                run("pe", e)

            @block.scalar
            def _(e):
                run("act", e)

            @block.vector
            def _(e):
                run("dve", e)

            @block.gpsimd
            def _(e):
                run("pool", e)

            @block.sync
            def _(e):
                run("sp", e)


class Arena:
    def __init__(self, ap, nwords):
        self.ap = ap
        self.n = nwords
        self.lo = 0
        self.hi = nwords

    def _view(self, off, words, shape, dtype, n):
        v = self.ap[:, off:off + words]
        if dtype != F32:
            v = v.bitcast(dtype)
        if v.shape[1] != n:
            v = v[:, 0:n]
        if len(shape) == 3:
            v = v.rearrange("p (a b) -> p a b", a=shape[1])
        elif len(shape) == 4:
            v = v.rearrange("p (a b c) -> p a b c", a=shape[1], b=shape[2])
        return v

    def alloc(self, shape, dtype, hi=True):
        n = 1
        for s in shape[1:]:
            n *= s
        esz = 4 if dtype in (F32, I32) else 2
        words = (n * esz + 3) // 4
        words = (words + 7) // 8 * 8
        assert self.lo + words <= self.hi, ("arena overflow", self.lo, self.hi, words)
        if hi:
            self.hi -= words
            off = self.hi
        else:
            off = self.lo
            self.lo += words
        self.last = (off, words)
        return self._view(off, words, shape, dtype, n)

    def lo_alloc(self, shape, dtype):
        return self.alloc(shape, dtype, hi=False)

    def mark(self):
        return (self.lo, self.hi)

    def reset(self, m):
        self.lo, self.hi = m

    def reset_hi(self, m):
        self.hi = m[1]

    def reset_lo(self, m):
        self.lo = m[0]


def bc(ap, shape):
    return ap.broadcast_to(list(shape))


def _cst_layout():
    names = [("ident", 128), ("tri", 128), ("ustr", 128), ("ones", 128), ("rotm", 128), ("amask", 256),
             ("amask0", 256), ("invf", 1), ("flag", 1), ("gmix", 16), ("gffn", 16), ("gple", 16), ("gssd", 16),
             ("convw", 96), ("convb", 24), ("dcol", 16), ("dtb", 32), ("alog", 32), ("sinks", 16), ("epsn", 1), ("epss", 1), ("one", 1)]
    off = {}
    o = 0
    for n, w in names:
        off[n] = (o, w)
        o += w
    return off, o


CST, CST_W = _cst_layout()


WSPEC = {"w_in": (D, IN_DIM), "w_attn_br": (Q_DIM, D), "w_ssd_br": (D, D), "w_o": (D, D), "w_gate": (D, FF),
         "w_up": (D, FF), "w_down": (FF, D), "w_ple_gate": (D, D), "w_ple_proj": (PLE, D)}


def build(NT, plan=None, stop_after=None):
    TK = NT * 128
    MW = min(512, TK)
    NG = TK // MW
    TPG = MW // 128
    nc = bass.Bass("TRN2", target_bir_lowering=False)

    def din(name, shape, dt=F32):
        return nc.dram_tensor(name, list(shape), dt, kind="ExternalInput").ap()

    x_d = din("x", [TK, D])
    xp_d = din("xprev", [TK, D])
    p_d = din("p", [TK, PLE])
    pos_d = din("pos", [128 + TK], I32)
    cst_d = din("cst", [128, CST_W])
    gfin_d = din("g_final", [D])
    WD = {k: din(k, list(v)) for k, v in WSPEC.items()}
    y_d = nc.dram_tensor("y", [TK, D], F32, kind="ExternalOutput").ap()

    def wview(spec):
        name, k0, k1, c0, c1 = spec
        return WD[name].rearrange("(k p) n -> p k n", p=128)[:, k0:k1, c0:c1]

    st = ExitStack()
    with st:
        P = Prog(nc, st)
        AW = 53000
        arena_t = st.enter_context(nc.sbuf_tensor("arena", [128, AW], F32))
        A = Arena(arena_t[:], AW)
        pbank = [st.enter_context(nc.psum_tensor("pb%d" % i, [128, 512], F32)) for i in range(8)]
        t_bank = P.tiles("bank", 8)
        rr = {"acc": 0, "aux": 0}

        def bank(pool):
            i = rr[pool]
            rr[pool] = (i + 1) % 4
            j = i + (0 if pool == "acc" else 4)
            return pbank[j], t_bank[j]

        DR = P.dram

        cst = A.lo_alloc([128, CST_W], F32)
        t_cst = P.tile("cst", dma=True)
        P.dma("sp", t_cst, DR, cst, cst_d)

        def C(name, a=None, b=None):
            o, w = CST[name]
            if a is None:
                return cst[:, o:o + w]
            return cst[:, o + a:o + b]

        identb = A.lo_alloc([128, 128], BF16)
        rotmb = A.lo_alloc([128, 128], BF16)
        t_cb = P.tile("cstb")
        P.I("dve", "tensor_copy", [t_cst], [t_cb], out=identb, in_=C("ident"))
        P.I("dve", "tensor_copy", [t_cst], [t_cb], out=rotmb, in_=C("rotm"))
        ustrb = A.lo_alloc([128, 128], BF16)
        onesb = A.lo_alloc([128, 128], BF16)
        trib = A.lo_alloc([128, 128], BF16)
        P.I("dve", "tensor_copy", [t_cst], [t_cb], out=ustrb, in_=C("ustr"))
        P.I("dve", "tensor_copy", [t_cst], [t_cb], out=onesb, in_=C("ones"))
        P.I("dve", "tensor_copy", [t_cst], [t_cb], out=trib, in_=C("tri"))
        aneg = A.lo_alloc([128, 32], F32)
        nsink = A.lo_alloc([128, 16], F32)
        P.I("act", "activation", [t_cst], [t_cb], out=aneg, in_=C("alog"), func=AF.Exp)
        P.I("dve", "tensor_scalar", [t_cb], [t_cb], out=aneg, in0=aneg, scalar1=-1.0, scalar2=None, op0=ALU.mult)
        P.I("dve", "tensor_scalar", [t_cst], [t_cb], out=nsink, in0=C("sinks"), scalar1=-1.0, scalar2=None, op0=ALU.mult)

        ws_stage = [A.lo_alloc([128, 4096], F32) for _ in range(2)]
        ws_bf = [A.lo_alloc([128, 4096], BF16) for _ in range(2)]
        t_stage = P.tiles("wstage", 2, dma=True)
        t_wbf = P.tiles("wbf", 2)
        ws = {"i": 0, "dma": 0, "cast": 0, "req": []}

        def ws_advance(i):
            n = len(plan)
            while True:
                progressed = False
                j = ws["dma"]
                if j < n and j <= i + 2 and (j < 2 or ws["cast"] > j - 2):
                    v = wview(plan[j])
                    kc, ncol = v.shape[1], v.shape[2]
                    s = j % 2
                    P.dma("sp", t_stage[s], DR, ws_stage[s][:, 0:kc * ncol].rearrange("p (k c) -> p k c", k=kc), v)
                    ws["dma"] += 1
                    progressed = True
                j = ws["cast"]
                if j < n and j <= i + 1 and j < ws["dma"]:
                    _, k0, k1, c0, c1 = plan[j]
                    sz = (k1 - k0) * (c1 - c0)
                    s = j % 2
                    P.I("act", "activation", [t_stage[s]], [t_wbf[s]], out=ws_bf[s][:, 0:sz], in_=ws_stage[s][:, 0:sz], func=AF.Copy)
                    ws["cast"] += 1
                    progressed = True
                if not progressed:
                    break

        def acquire(name, k0, k1, c0, c1):
            spec = (name, k0, k1, c0, c1)
            kc, ncol = k1 - k0, c1 - c0
            assert kc * ncol <= 4096
            i = ws["i"]
            ws["i"] += 1
            if plan is None:
                ws["req"].append(spec)
            else:
                assert plan[i] == spec, (i, plan[i], spec)
                ws_advance(i)
            s = i % 2
            return ws_bf[s][:, 0:kc * ncol].rearrange("p (k c) -> p k c", k=kc), t_wbf[s]

        UT = A.lo_alloc([128, KC, TK], BF16)
        t_UT = P.tiles("UT", NT)
        stat = A.lo_alloc([128, 64], F32)
        t_stat = P.tiles("stat", 16)
        stat_i = [0]

        def stat_slot():
            i = stat_i[0]
            stat_i[0] = (i + 1) % 16
            return stat[:, 4 * i:4 * i + 4], t_stat[i]

        BASE = A.mark()

        class _Stop(Exception):
            pass

        PH = []

        def stop_here(tag):
            PH.append((tag, sum(1 for o in P.ops["pe"] if o[1] is not None)))
            if stop_after == tag:
                P.barrier(hard=True)
                raise _Stop()

        try:
            def norm_stage1(src, t_src, ub, t_ub, junk, t_junk):
                ss, t_ss = stat_slot()
                P.I("dve", "memset", [], [t_ss], ap=ss[:, 0:1], constant=0.0)
                P.I("act", "activation", [t_src, t_ss], [t_junk, t_ss], out=junk, in_=src, func=AF.Square, accum_out=ss[:, 0:1])
                P.I("act", "activation", [t_ss], [t_ss], out=ss[:, 1:2], in_=ss[:, 0:1], func=AF.Sqrt, scale=1.0 / D, bias=C("epsn"))
                P.I("dve", "reciprocal", [t_ss], [t_ss], out=ss[:, 2:3], in_=ss[:, 1:2])
                P.I("act", "activation", [t_src, t_ss], [t_ub], out=ub, in_=src, func=AF.Copy, scale=ss[:, 2:3])

            def norm_stage2(gname, ti, ub, t_ub):
                for hb in range(2):
                    pb, tb = bank("aux")
                    pv = pb[:].bitcast(BF16)
                    for k in range(8):
                        kc = hb * 8 + k
                        P.I("pe", "transpose", [t_ub, t_cb], [tb], inc=(k == 7), out=pv[:, k * 128:(k + 1) * 128],
                            in_=ub[:, kc * 128:(kc + 1) * 128], identity=identb)
                    P.I("dve", "tensor_tensor", [tb, t_cst], [t_UT[ti]], out=UT[:, hb * 8:(hb + 1) * 8, ti * 128:(ti + 1) * 128],
                        in0=pv.rearrange("p (k t) -> p k t", k=8),
                        in1=bc(C(gname, hb * 8, hb * 8 + 8).unsqueeze(2), [128, 8, 128]), op=ALU.mult)

            def norm_ctx(nslots):
                ub = [A.alloc([128, D], BF16) for _ in range(nslots)]
                t_ub = P.tiles("ubx", nslots)
                junk = A.alloc([128, D], BF16)
                t_junk = P.tile("junkx")
                return {"ub": ub, "t_ub": t_ub, "junk": junk, "t_junk": t_junk, "n": nslots}

            def norm_s1(ctx, src, t_src, ti):
                s = ti % ctx["n"]
                norm_stage1(src, t_src, ctx["ub"][s], ctx["t_ub"][s], ctx["junk"], ctx["t_junk"])

            def norm_s2(ctx, gname, ti):
                s = ti % ctx["n"]
                norm_stage2(gname, ti, ctx["ub"][s], ctx["t_ub"][s])

            def proj_ws(wb, t_wb, cb, nk, srcT, t_src_list, tg, pb, tb):
                for k in range(nk):
                    P.I("pe", "matmul", [t_wb] + t_src_list, [tb], inc=(k == nk - 1), out=pb[:, 0:MW],
                        lhsT=wb[:, k, cb * 128:(cb + 1) * 128], rhs=srcT[:, k, tg * MW:(tg + 1) * MW],
                        start=(k == 0), stop=(k == nk - 1))

            def tg_tiles(tlist, tg):
                return tlist[tg * TPG:(tg + 1) * TPG]

            def norm_from_dram(x_dram, gname):
                m = A.mark()
                xs = [A.alloc([128, D], F32) for _ in range(2)]
                t_xs = P.tiles("xs", 2, dma=True)
                ub = [A.alloc([128, D], BF16) for _ in range(2)]
                t_ub = P.tiles("ub", 2)
                junk = A.alloc([128, D], BF16)
                t_junk = P.tile("junk")

                def s1(ti):
                    s = ti % 2
                    P.dma("sp", t_xs[s], DR, xs[s], x_dram[ti * 128:(ti + 1) * 128, :])
                    norm_stage1(xs[s], t_xs[s], ub[s], t_ub[s], junk, t_junk)
                s1(0)
                for ti in range(NT):
                    if ti + 1 < NT:
                        s1(ti + 1)
                    norm_stage2(gname, ti, ub[ti % 2], t_ub[ti % 2])
                A.reset(m)
                P.barrier()

            def trig_tables(c0, n, cosT, sinT, t_trig):
                m = A.mark()
                posi = A.alloc([128, n], I32)
                ang = A.alloc([128, n], F32)
                kq = A.alloc([128, n], F32)
                ki = A.alloc([128, n], I32)
                t_pos = P.tile("pos", dma=True)
                t_tmp = P.tile("trigtmp")
                P.dma("sp", t_pos, DR, posi, pos_d[c0:c0 + n].partition_broadcast(128))
                P.I("dve", "tensor_copy", [t_pos], [t_tmp], out=ang, in_=posi)
                P.I("dve", "tensor_scalar", [t_tmp, t_cst], [t_tmp], out=ang, in0=ang, scalar1=C("invf"), scalar2=None, op0=ALU.mult)
                C1 = 6.28125
                C2 = 2.0 * math.pi - C1
                for dst, shift in ((sinT, 0.0), (cosT, math.pi / 2.0)):
                    wr = [t_tmp, t_trig]
                    P.I("dve", "tensor_scalar", [t_tmp], [t_tmp], out=kq, in0=ang, scalar1=shift, scalar2=1.0 / (2.0 * math.pi), op0=ALU.add, op1=ALU.mult)
                    P.I("dve", "tensor_copy", [t_tmp], [t_tmp], out=ki, in_=kq)
                    P.I("dve", "tensor_copy", [t_tmp], [t_tmp], out=kq, in_=ki)
                    P.I("dve", "scalar_tensor_tensor", [t_tmp], wr, out=dst, in0=kq, scalar=-C1, in1=ang, op0=ALU.mult, op1=ALU.add)
                    P.I("dve", "scalar_tensor_tensor", [t_tmp, t_trig], wr, out=dst, in0=kq, scalar=-C2, in1=dst, op0=ALU.mult, op1=ALU.add)
                    if shift != 0.0:
                        P.I("dve", "tensor_scalar", [t_trig], wr, out=dst, in0=dst, scalar1=shift, scalar2=None, op0=ALU.add)
                    P.I("dve", "tensor_scalar", [t_trig], wr, out=kq, in0=dst, scalar1=math.pi, scalar2=-2.0 * math.pi, op0=ALU.is_gt, op1=ALU.mult)
                    P.I("dve", "tensor_tensor", [t_tmp, t_trig], wr, out=dst, in0=dst, in1=kq, op=ALU.add)
                    P.I("dve", "tensor_scalar", [t_trig], wr, out=kq, in0=dst, scalar1=-math.pi, scalar2=2.0 * math.pi, op0=ALU.is_lt, op1=ALU.mult)
                    P.I("dve", "tensor_tensor", [t_tmp, t_trig], wr, out=dst, in0=dst, in1=kq, op=ALU.add)
                    P.I("dve", "tensor_scalar", [t_trig], wr, out=dst, in0=dst, scalar1=math.pi, scalar2=-math.pi, op0=ALU.min, op1=ALU.max)
                    P.I("act", "activation", [t_trig], wr, out=dst, in_=dst, func=AF.Sin)
                A.reset(m)

            def rope_evac(pb, tb, dst, t_dst, ncols, cosv, sinv, t_trig, scr, t_scr):
                qb, r1, r2 = scr
                P.I("act", "activation", [tb], [t_scr], out=qb[:, 0:ncols], in_=pb[:, 0:ncols], func=AF.Copy)
                pb2, tb2 = bank("aux")
                P.I("pe", "matmul", [t_scr, t_cb], [tb2], out=pb2[:, 0:ncols], lhsT=rotmb, rhs=qb[:, 0:ncols], start=True, stop=True)
                P.I("dve", "tensor_tensor", [tb, t_trig], [t_scr], out=r1[:, 0:ncols], in0=pb[:, 0:ncols], in1=cosv, op=ALU.mult)
                P.I("dve", "tensor_tensor", [tb2, t_trig], [t_scr], out=r2[:, 0:ncols], in0=pb2[:, 0:ncols], in1=sinv, op=ALU.mult)
                P.I("dve", "tensor_tensor", [t_scr], t_dst, out=dst, in0=r1[:, 0:ncols], in1=r2[:, 0:ncols], op=ALU.add)

            attnT = A.lo_alloc([128, 8, TK], BF16)
            t_attnT = P.tiles("attnT", NT)
            LO_A = A.mark()
            XBX = A.lo_alloc([128, 16, TK], BF16)
            t_XBX = [[P.tile("xbx%d_%d" % (b, g)) for g in range(NG)] for b in range(16)]
            S32 = A.alloc([128, 4, 512], F32)
            Sbf = A.alloc([128, 4, 512], BF16)
            t_S = P.tiles("S", 4)
            t_Sbf = P.tiles("Sbf", 4)
            halo = A.alloc([128, 24, 4], F32)
            t_halo = P.tiles("halo", 24)
            HI_S = A.mark()
            KTd = A.alloc([128, 4, 128 + TK], BF16)
            _ktd_off = A.last
            t_KT = P.tiles("KT", NT + 1)
            Vt = A.alloc([128, NT + 1, 256], BF16)
            _vt_off = A.last
            t_V = P.tiles("V", NT + 1)
            HI_KV = A.mark()
            DBG = {}

            for g in range(4):
                P.I("dve", "memset", [], [t_S[g]], ap=S32[:, g, :], constant=0.0)
                P.I("dve", "memset", [], [t_Sbf[g]], ap=Sbf[:, g, :], constant=0.0)
            for b in range(24):
                P.I("pool", "memset", [], [t_halo[b]], ap=halo[:, b, :], constant=0.0)

            def k_proj(tiles_mode, cosT, sinT, t_trig, scr, t_scr, scrs=None):
                wk, t_wk = acquire("w_in", 0, KC, OFF_K, OFF_K + 256)
                halo_m = tiles_mode == "halo"
                kunits = [(kvh, -1) for kvh in range(4)] if halo_m else [(kvh, tg) for kvh in range(4) for tg in range(NG)]
                kst = {}

                def geom(tg):
                    if tg < 0:
                        return 128, slice((NT - 1) * 128, NT * 128), [t_UT[NT - 1]], slice(0, 128), slice(0, 128), [t_KT[0]]
                    return (MW, slice(tg * MW, (tg + 1) * MW), tg_tiles(t_UT, tg), slice(tg * MW, (tg + 1) * MW),
                            slice(128 + tg * MW, 128 + (tg + 1) * MW), t_KT[1 + tg * TPG:1 + (tg + 1) * TPG])

                def k_head(u):
                    kvh, tg = kunits[u]
                    n, ucols, tu, tcols, kcols, tk = geom(tg)
                    pb, tb = bank("acc")
                    for k in range(KC):
                        for half in range(2):
                            P.I("pe", "matmul", [t_wk] + tu, [tb], inc=(k == KC - 1 and half == 1),
                                out=pb[half * 64:(half + 1) * 64, 0:n], lhsT=wk[:, k, kvh * 64:(kvh + 1) * 64],
                                rhs=UT[:, k, ucols], start=(k == 0), stop=(k == KC - 1))
                    sc_, tsc_ = scrs[u % 2]
                    P.I("act", "activation", [tb], [tsc_], out=sc_[0][:, 0:n], in_=pb[:, 0:n], func=AF.Copy)
                    kst[u] = (pb, tb)

                def k_tail(u):
                    kvh, tg = kunits[u]
                    n, ucols, tu, tcols, kcols, tk = geom(tg)
                    pb, tb = kst.pop(u)
                    (qb, r1, r2), tsc_ = scrs[u % 2]
                    pb2, tb2 = bank("aux")
                    P.I("pe", "matmul", [tsc_, t_cb], [tb2], out=pb2[:, 0:n], lhsT=rotmb, rhs=qb[:, 0:n], start=True, stop=True)
                    P.I("dve", "tensor_tensor", [tb, t_trig], [tsc_], out=r1[:, 0:n], in0=pb[:, 0:n], in1=cosT[:, tcols], op=ALU.mult)
                    P.I("dve", "tensor_tensor", [tb2, t_trig], [tsc_], out=r2[:, 0:n], in0=pb2[:, 0:n], in1=sinT[:, tcols], op=ALU.mult)
                    P.I("dve", "tensor_tensor", [tsc_], tk, out=KTd[:, kvh, kcols], in0=r1[:, 0:n], in1=r2[:, 0:n], op=ALU.add)

                k_head(0)
                for u in range(len(kunits)):
                    if u + 1 < len(kunits):
                        k_head(u + 1)
                    k_tail(u)

            def v_proj(tiles_mode):
                wv, t_wv = acquire("w_in", 0, KC, OFF_V, OFF_V + 256)
                vt = [NT - 1] if tiles_mode == "halo" else list(range(NT))
                for ti in vt:
                    pb, tb = bank("acc")
                    for k in range(KC):
                        P.I("pe", "matmul", [t_wv, t_UT[ti]], [tb], inc=(k == KC - 1), out=pb[:, 0:256],
                            lhsT=UT[:, k, ti * 128:(ti + 1) * 128], rhs=wv[:, k, :], start=(k == 0), stop=(k == KC - 1))
                    slot = 0 if tiles_mode == "halo" else ti + 1
                    P.I("act", "activation", [tb], [t_V[slot]], out=Vt[:, slot, :], in_=pb[:, 0:256], func=AF.Copy)

            def dt_path(dtv, t_dtv, abf=None, fold_suffix=False):
                wd, t_wd = acquire("w_in", 0, KC, OFF_DT, OFF_DT + 32)
                pb, tb = bank("aux")
                for ti in range(NT):
                    for k in range(KC):
                        P.I("pe", "matmul", [t_wd, t_UT[ti]], [tb], inc=(k == KC - 1), out=pb[:, ti * 32:(ti + 1) * 32],
                            lhsT=UT[:, k, ti * 128:(ti + 1) * 128], rhs=wd[:, k, :], start=(k == 0), stop=(k == KC - 1))
                v = lambda i: dtv[:, i, :, :]
                raw = pb[:, 0:NT * 32].rearrange("p (t h) -> p t h", t=NT)
                P.I("dve", "tensor_tensor", [tb, t_cst], [t_dtv], out=v(7), in0=raw, in1=bc(C("dtb").unsqueeze(1), [128, NT, 32]), op=ALU.add)
                P.I("dve", "tensor_scalar", [t_dtv], [t_dtv], out=v(7), in0=v(7), scalar1=80.0, scalar2=None, op0=ALU.min)
                P.I("act", "activation", [t_dtv], [t_dtv], out=v(7), in_=v(7), func=AF.Exp)
                P.I("act", "activation", [t_dtv], [t_dtv], out=v(0), in_=v(7), func=AF.Ln, bias=C("one"))
                P.I("dve", "tensor_tensor", [t_dtv, t_cb], [t_dtv], out=v(1), in0=v(0), in1=bc(aneg.unsqueeze(1), [128, NT, 32]), op=ALU.mult)
                pb2, tb2 = bank("aux")
                for ti in range(NT):
                    P.I("pe", "matmul", [t_dtv, t_cst], [tb2], inc=False, out=pb2[:, ti * 64:ti * 64 + 32], lhsT=C("tri"), rhs=dtv[:, 1, ti, :], start=True, stop=True)
                    P.I("pe", "matmul", [t_dtv, t_cst], [tb2], inc=(ti == NT - 1), out=pb2[:, ti * 64 + 32:ti * 64 + 64], lhsT=C("ones"), rhs=dtv[:, 1, ti, :], start=True, stop=True)
                cc = pb2[:, 0:NT * 64].rearrange("p (t c) -> p t c", t=NT)
                P.I("act", "activation", [tb2], [t_dtv], out=v(2), in_=cc[:, :, 0:32], func=AF.Copy)
                P.I("act", "activation", [tb2], [t_dtv], out=v(3), in_=cc[:, :, 32:64], func=AF.Copy)
                P.I("dve", "tensor_tensor", [t_dtv], [t_dtv], out=v(7), in0=v(3), in1=v(2), op=ALU.subtract)
                P.I("act", "activation", [t_dtv], [t_dtv], out=v(4), in_=v(7), func=AF.Exp)
                P.I("act", "activation", [t_dtv], [t_dtv], out=v(5), in_=v(3), func=AF.Exp)
                P.I("dve", "tensor_tensor", [t_dtv], [t_dtv], out=v(6), in0=v(0), in1=v(4), op=ALU.mult)
                if fold_suffix:
                    P.I("dve", "memset", [], [t_dtv], ap=dtv[:, 7, NT - 1, :], constant=0.0)
                    for c in range(NT - 2, -1, -1):
                        P.I("dve", "tensor_tensor", [t_dtv], [t_dtv], out=dtv[:, 7, c, :], in0=dtv[:, 7, c + 1, :], in1=dtv[:, 3, c + 1, :], op=ALU.add)
                    P.I("act", "activation", [t_dtv], [t_dtv], out=v(7), in_=v(7), func=AF.Exp)
                    P.I("dve", "tensor_tensor", [t_dtv], [t_dtv], out=v(6), in0=v(6), in1=v(7), op=ALU.mult)
                if abf is not None:
                    P.I("dve", "tensor_copy", [t_dtv], [t_dtv], out=abf, in_=v(1))
                    P.I("dve", "tensor_tensor", [t_dtv], [t_dtv], out=v(2), in0=v(1), in1=abf, op=ALU.subtract)

            def xbc_path(XBB, t_XBB, full):
                m = A.mark()
                pre = [A.alloc([128, 4 + MW], F32) for _ in range(2)]
                t_pre = P.tiles("pre", 2)
                acc = [A.alloc([128, MW], F32) for _ in range(2)]
                t_accb = P.tiles("cacc", 2)
                units = []
                for pc in range(12):
                    for cb in range(2):
                        if pc >= 10 and not full:
                            units.append((pc, cb, -1))
                        else:
                            for tg in range(NG):
                                units.append((pc, cb, tg))
                stt = {"n": 0}

                def head(u):
                    pc, cb, tg = units[u]
                    blk = pc * 2 + cb
                    if cb == 0 and tg <= 0:
                        stt["w"] = acquire("w_in", 0, KC, OFF_XBC + pc * 256, OFF_XBC + (pc + 1) * 256)
                    wp, t_wp = stt["w"]
                    if tg < 0:
                        pb, tb = bank("acc")
                        for k in range(KC):
                            P.I("pe", "matmul", [t_wp, t_UT[NT - 1]], [tb], inc=(k == KC - 1), out=pb[:, 0:128],
                                lhsT=wp[:, k, cb * 128:(cb + 1) * 128], rhs=UT[:, k, (NT - 1) * 128:NT * 128], start=(k == 0), stop=(k == KC - 1))
                        P.I("act", "activation", [tb], [t_halo[blk]], out=halo[:, blk, 0:3], in_=pb[:, 125:128], func=AF.Copy)
                        return
                    pb, tb = bank("acc")
                    proj_ws(wp, t_wp, cb, KC, UT, tg_tiles(t_UT, tg), tg, pb, tb)
                    s = stt["n"] % 2
                    stt["n"] += 1
                    stt[u] = s
                    P.I("pool", "tensor_copy", [t_halo[blk]], [t_pre[s]], out=pre[s][:, 0:3], in_=halo[:, blk, 0:3])
                    P.I("act", "activation", [tb], [t_pre[s]], out=pre[s][:, 3:3 + MW], in_=pb[:, 0:MW], func=AF.Copy)
                    P.I("pool", "tensor_copy", [t_pre[s]], [t_halo[blk]], out=halo[:, blk, 0:3], in_=pre[s][:, MW:MW + 3])

                def tail(u):
                    pc, cb, tg = units[u]
                    if tg < 0:
                        return
                    blk = pc * 2 + cb
                    s = stt.pop(u)
                    cw = lambda j: C("convw", blk * 4 + j, blk * 4 + j + 1)
                    P.I("dve", "tensor_scalar", [t_pre[s], t_cst], [t_accb[s]], out=acc[s], in0=pre[s][:, 0:MW], scalar1=cw(0), scalar2=None, op0=ALU.mult)
                    for j in range(1, 4):
                        P.I("dve", "scalar_tensor_tensor", [t_pre[s], t_cst, t_accb[s]], [t_accb[s]], out=acc[s], in0=pre[s][:, j:j + MW],
                            scalar=cw(j), in1=acc[s], op0=ALU.mult, op1=ALU.add)
                    if blk < 16:
                        dst, tdst = XBX[:, blk, tg * MW:(tg + 1) * MW], t_XBX[blk][tg]
                    else:
                        dst, tdst = XBB[:, blk - 16, tg * MW:(tg + 1) * MW], t_XBB[blk - 16][tg]
                    P.I("act", "activation", [t_accb[s], t_cst], [tdst], out=dst, in_=acc[s], func=AF.Silu, bias=C("convb", blk, blk + 1))

                head(0)
                for u in range(len(units)):
                    if u + 1 < len(units):
                        head(u + 1)
                    tail(u)
                A.reset(m)
                P.barrier()

            def ssd_chunks(XBB, t_XBB, dtv, t_dtv, is_prefix, abf=None):
                m = A.mark()
                xd = A.alloc([128, D], BF16)
                xdw = A.alloc([128, D], BF16)
                t_xd = P.tile("xd")
                t_xdw = P.tile("xdw")
                Btok = A.alloc([128, 4, 128], BF16)
                t_Btok = P.tile("Btok")
                if not is_prefix:
                    cbm = A.alloc([128, 4, 128], BF16)
                    t_cbm = P.tile("cbm")
                    ypre = [A.alloc([128, 4, 128], F32) for _ in range(2)]
                    t_ypre = P.tiles("ypre", 2)
                    NAT = 4
                    ATh = [A.alloc([128, 4, 128], BF16) for _ in range(NAT)]
                    ATl = [A.alloc([128, 4, 128], BF16) for _ in range(NAT)]
                    t_AT = P.tiles("AT", NAT)
                    eD = [A.alloc([128, 4, 128], BF16) for _ in range(2)]
                    eE = [A.alloc([128, 4, 128], BF16) for _ in range(2)]
                    t_eD = P.tiles("eD", 2)
                    t_eE = P.tiles("eE", 2)
                    MT = [A.alloc([128, 4, 128], BF16) for _ in range(2)]
                    CsT = [A.alloc([128, 4, 128], BF16) for _ in range(2)]
                    t_MT = P.tiles("MT", 2)
                    t_CsT = P.tiles("CsT", 2)
                if is_prefix:
                    sbanks = [bank("acc") for _ in range(4)]
                for c in range(NT):
                    tg = c // TPG
                    cs = slice(c * 128, (c + 1) * 128)
                    for hb in range(2):
                        pb, tb = bank("aux")
                        pv = pb[:].bitcast(BF16)
                        for k in range(8):
                            fb = hb * 8 + k
                            P.I("pe", "transpose", [t_XBX[fb][tg], t_cb], [tb], inc=(k == 7), out=pv[:, k * 128:(k + 1) * 128],
                                in_=XBX[:, fb, cs], identity=identb)
                        src = pv.rearrange("p (h d) -> p h d", h=16)
                        hs = slice(hb * 16, (hb + 1) * 16)
                        if not is_prefix:
                            P.I("dve", "tensor_tensor", [tb, t_dtv], [t_xd], out=xd[:, hb * 1024:(hb + 1) * 1024].rearrange("p (h d) -> p h d", h=16),
                                in0=src, in1=bc(dtv[:, 0, c, hs].unsqueeze(2), [128, 16, 64]), op=ALU.mult)
                        P.I("dve", "tensor_tensor", [tb, t_dtv], [t_xdw], out=xdw[:, hb * 1024:(hb + 1) * 1024].rearrange("p (h d) -> p h d", h=16),
                            in0=src, in1=bc(dtv[:, 6, c, hs].unsqueeze(2), [128, 16, 64]), op=ALU.mult)
                    pb, tb = bank("aux")
                    pv = pb[:].bitcast(BF16)
                    for g in range(4):
                        P.I("pe", "transpose", [t_XBB[g][tg], t_cb], [tb], inc=(g == 3), out=pv[:, g * 128:(g + 1) * 128],
                            in_=XBB[:, g, cs], identity=identb)
                    P.I("act", "activation", [tb], [t_Btok], out=Btok, in_=pv[:, 0:512].rearrange("p (g n) -> p g n", g=4), func=AF.Copy)
                    if not is_prefix:
                        pb, tb = bank("aux")
                        for g in range(4):
                            P.I("pe", "matmul", [t_XBB[g][tg], t_XBB[4 + g][tg]], [tb], inc=(g == 3), out=pb[:, g * 128:(g + 1) * 128],
                                lhsT=XBB[:, g, cs], rhs=XBB[:, 4 + g, cs], start=True, stop=True)
                        P.I("dve", "tensor_tensor", [tb, t_cst], [t_cbm], out=cbm, in0=pb[:, 0:512].rearrange("p (g l) -> p g l", g=4),
                            in1=bc(C("tri").unsqueeze(1), [128, 4, 128]), op=ALU.mult)
                        ybanks = [bank("acc") for _ in range(4)]

                        def qX(hq):
                            g = hq // 2
                            s = hq % 2
                            sa = (c * 8 + hq) % NAT
                            trb = bc(trib.unsqueeze(1), [128, 4, 128])
                            P.I("pool", "tensor_tensor", [t_dtv, t_cb], [t_AT[sa]], out=ATh[sa], in0=bc(abf[:, c, hq * 4:(hq + 1) * 4].unsqueeze(2), [128, 4, 128]), in1=trb, op=ALU.mult)
                            P.I("pool", "tensor_tensor", [t_dtv, t_cb], [t_AT[sa]], out=ATl[sa], in0=bc(dtv[:, 2, c, hq * 4:(hq + 1) * 4].unsqueeze(2), [128, 4, 128]), in1=trb, op=ALU.mult)
                            pbD, tbD = bank("aux")
                            P.I("pe", "matmul", [t_AT[sa], t_cb], [tbD], inc=False, out=pbD[:, 0:512], lhsT=ustrb, rhs=ATh[sa].rearrange("p h l -> p (h l)"), start=True, stop=False)
                            P.I("pe", "matmul", [t_AT[sa], t_cb], [tbD], out=pbD[:, 0:512], lhsT=ustrb, rhs=ATl[sa].rearrange("p h l -> p (h l)"), start=False, stop=True)
                            pbE, tbE = bank("aux")
                            P.I("pe", "matmul", [t_AT[sa], t_cb], [tbE], inc=False, out=pbE[:, 0:512], lhsT=onesb, rhs=ATh[sa].rearrange("p h l -> p (h l)"), start=True, stop=False)
                            P.I("pe", "matmul", [t_AT[sa], t_cb], [tbE], out=pbE[:, 0:512], lhsT=onesb, rhs=ATl[sa].rearrange("p h l -> p (h l)"), start=False, stop=True)
                            P.I("act", "activation", [tbD], [t_eD[s]], out=eD[s].rearrange("p h l -> p (h l)"), in_=pbD[:, 0:512], func=AF.Exp)
                            P.I("act", "activation", [tbE], [t_eE[s]], out=eE[s].rearrange("p h l -> p (h l)"), in_=pbE[:, 0:512], func=AF.Exp)
                            P.I("dve", "tensor_tensor", [t_eD[s], t_cbm], [t_MT[s]], out=MT[s], in0=eD[s], in1=bc(cbm[:, g, :].unsqueeze(1), [128, 4, 128]), op=ALU.mult)
                            P.I("dve", "tensor_tensor", [t_eE[s], t_XBB[4 + g][tg]], [t_CsT[s]], out=CsT[s], in0=eE[s],
                                in1=bc(XBB[:, 4 + g, cs].unsqueeze(1), [128, 4, 128]), op=ALU.mult)

                        def qY(hq):
                            g = hq // 2
                            s = hq % 2
                            for hl in range(4):
                                h = hq * 4 + hl
                                pair = h // 2
                                pby, tby = ybanks[pair // 4]
                                o = pby[(h % 2) * 64:(h % 2 + 1) * 64, (pair % 4) * 128:(pair % 4 + 1) * 128]
                                P.I("pe", "matmul", [t_xd, t_MT[s]], [tby], inc=False, out=o, lhsT=xd[:, h * 64:(h + 1) * 64], rhs=MT[s][:, hl, :], start=True, stop=False)
                                P.I("pe", "matmul", [t_Sbf[g], t_CsT[s]], [tby], inc=(h % 8 == 7), out=o, lhsT=Sbf[:, g, (h % 8) * 64:(h % 8 + 1) * 64],
                                    rhs=CsT[s][:, hl, :], start=False, stop=True)

                        qX(0)
                        for hq in range(8):
                            if hq + 1 < 8:
                                qX(hq + 1)
                            qY(hq)
                        for b4 in range(4):
                            pby, tby = ybanks[b4]
                            ys = b4 % 2
                            xv = XBX[:, b4 * 4:(b4 + 1) * 4, cs]
                            xt = [t_XBX[fb][tg] for fb in range(b4 * 4, b4 * 4 + 4)]
                            P.I("pool", "tensor_tensor", xt + [t_cst], [t_ypre[ys]], out=ypre[ys], in0=xv,
                                in1=bc(C("dcol", b4 * 4, b4 * 4 + 4).unsqueeze(2), [128, 4, 128]), op=ALU.mult)
                            P.I("dve", "tensor_tensor", [t_ypre[ys], tby], xt, out=xv, in0=ypre[ys], in1=pby[:, 0:512].rearrange("p (a l) -> p a l", a=4), op=ALU.add)
                    if is_prefix:
                        for g in range(4):
                            pb, tb = sbanks[g]
                            P.I("pe", "matmul", [t_Btok, t_xdw], [tb], inc=(c == NT - 1), out=pb[:, 0:512], lhsT=Btok[:, g, :], rhs=xdw[:, g * 512:(g + 1) * 512],
                                start=(c == 0), stop=(c == NT - 1))
                        continue
                    for g in range(4):
                        pb, tb = bank("aux")
                        P.I("pe", "matmul", [t_Btok, t_xdw], [tb], out=pb[:, 0:512], lhsT=Btok[:, g, :], rhs=xdw[:, g * 512:(g + 1) * 512], start=True, stop=True)
                        sv = S32[:, g, :].rearrange("p (h d) -> p h d", h=8)
                        P.I("pool", "tensor_tensor", [t_S[g], t_dtv], [t_S[g]], out=sv, in0=sv, in1=bc(dtv[:, 5, c, g * 8:(g + 1) * 8].unsqueeze(2), [128, 8, 64]), op=ALU.mult)
                        P.I("dve", "tensor_tensor", [t_S[g], tb], [t_S[g]], out=S32[:, g, :], in0=S32[:, g, :], in1=pb[:, 0:512], op=ALU.add)
                        if not is_prefix:
                            if c < NT - 1:
                                P.I("act", "activation", [t_S[g]], [t_Sbf[g]], out=Sbf[:, g, :], in_=S32[:, g, :], func=AF.Copy)
                if is_prefix:
                    for g in range(4):
                        pb, tb = sbanks[g]
                        P.I("dve", "tensor_scalar", [tb, t_cst], [t_S[g]], out=S32[:, g, :], in0=pb[:, 0:512], scalar1=C("flag"), scalar2=None, op0=ALU.mult)
                        P.I("act", "activation", [t_S[g]], [t_Sbf[g]], out=Sbf[:, g, :], in_=S32[:, g, :], func=AF.Copy)
                A.reset(m)

            scr = (A.alloc([128, MW], BF16), A.alloc([128, MW], F32), A.alloc([128, MW], F32))
            t_scr = P.tile("scr")
            norm_from_dram(xp_d, "gmix")
            stop_here("norm1")
            m1 = A.mark()
            cosH = A.alloc([128, 128], F32)
            _cosh_off = A.last
            sinH = A.alloc([128, 128], F32)
            _sinh_off = A.last
            t_trigH = P.tile("trigH")
            trig_tables(0, 128, cosH, sinH, t_trigH)
            scrh = (A.alloc([128, 128], BF16), A.alloc([128, 128], F32), A.alloc([128, 128], F32))
            t_scrh = P.tile("scrh")
            k_proj("halo", cosH, sinH, t_trigH, scr, t_scr, [(scr, t_scr), (scrh, t_scrh)])
            if stop_after == "khalo":
                P.barrier()
                P.emit()
                return nc, {"ktd": _ktd_off, "vt": _vt_off, "cosh": _cosh_off, "sinh": _sinh_off}
            v_proj("halo")
            stop_here("kvhalo")
            A.reset(m1)
            XBB = A.alloc([128, 8, TK], BF16)
            t_XBB = [[P.tile("xbb%d_%d" % (b, g)) for g in range(NG)] for b in range(8)]
            dtv = A.alloc([128, 8, NT, 32], F32)
            t_dtv = P.tile("dtv")
            P.barrier()
            dt_path(dtv, t_dtv, fold_suffix=True)
            stop_here("dt1")
            xbc_path(XBB, t_XBB, False)
            stop_here("xbc1")
            ssd_chunks(XBB, t_XBB, dtv, t_dtv, True)
            stop_here("ssd1")
            A.reset_hi(HI_KV)
            A.reset_lo(LO_A)
            P.barrier()
            if stop_after == "prefix":
                P.emit()
                return nc, {"ktd": _ktd_off, "vt": _vt_off}

            scr = (A.alloc([128, MW], BF16), A.alloc([128, MW], F32), A.alloc([128, MW], F32))
            t_scr = P.tile("scr2")
            scr2 = (A.alloc([128, MW], BF16), A.alloc([128, MW], F32), A.alloc([128, MW], F32))
            t_scr2 = P.tile("scr3")
            scrs = [(scr, t_scr), (scr2, t_scr2)]
            QT = A.alloc([128, 8, TK], BF16)
            t_QT = P.tiles("QT", NT)
            norm_from_dram(x_d, "gmix")
            stop_here("norm2")
            m1 = A.mark()
            cosT = A.alloc([128, TK], F32)
            sinT = A.alloc([128, TK], F32)
            t_trig = P.tile("trig")
            trig_tables(128, TK, cosT, sinT, t_trig)
            k_proj("main", cosT, sinT, t_trig, scr, t_scr, scrs)
            v_proj("main")
            stop_here("kvmain")
            qunits = [(pc, cb, tg) for pc in range(4) for cb in range(2) for tg in range(NG)]
            qst = {}

            def q_head(u):
                pc, cb, tg = qunits[u]
                if (cb, tg) == (0, 0):
                    qst["w"] = acquire("w_in", 0, KC, OFF_Q + pc * 256, OFF_Q + (pc + 1) * 256)
                wq, t_wq = qst["w"]
                pb, tb = bank("acc")
                proj_ws(wq, t_wq, cb, KC, UT, tg_tiles(t_UT, tg), tg, pb, tb)
                sc_, tsc_ = scrs[u % 2]
                P.I("act", "activation", [tb], [tsc_], out=sc_[0][:, 0:MW], in_=pb[:, 0:MW], func=AF.Copy)
                qst[u] = (pb, tb)

            def q_tail(u):
                pc, cb, tg = qunits[u]
                hp = pc * 2 + cb
                pb, tb = qst.pop(u)
                (qb, r1, r2), tsc_ = scrs[u % 2]
                pb2, tb2 = bank("aux")
                cosv, sinv = cosT[:, tg * MW:(tg + 1) * MW], sinT[:, tg * MW:(tg + 1) * MW]
                P.I("pe", "matmul", [tsc_, t_cb], [tb2], out=pb2[:, 0:MW], lhsT=rotmb, rhs=qb[:, 0:MW], start=True, stop=True)
                P.I("dve", "tensor_tensor", [tb, t_trig], [tsc_], out=r1[:, 0:MW], in0=pb[:, 0:MW], in1=cosv, op=ALU.mult)
                P.I("dve", "tensor_tensor", [tb2, t_trig], [tsc_], out=r2[:, 0:MW], in0=pb2[:, 0:MW], in1=sinv, op=ALU.mult)
                P.I("dve", "tensor_tensor", [tsc_], tg_tiles(t_QT, tg), out=QT[:, hp, tg * MW:(tg + 1) * MW], in0=r1[:, 0:MW], in1=r2[:, 0:MW], op=ALU.add)

            q_head(0)
            for u in range(len(qunits)):
                if u + 1 < len(qunits):
                    q_head(u + 1)
                q_tail(u)
            A.reset(m1)
            P.barrier()

            stop_here("qkv")
            sc = [A.alloc([128, 16, 256], F32) for _ in range(2)]
            t_sc = P.tiles("sc", 2)
            Pn = A.alloc([128, 16, 256], BF16)
            t_Pn = P.tile("Pn")
            PT = A.alloc([128, 32, 128], BF16)
            t_PT = P.tiles("PT", 4)
            ast = [A.alloc([128, 6, 16], F32) for _ in range(2)]
            t_ast = P.tiles("ast", 2)
            def att_A(j):
                s = j % 2
                mk = C("amask0") if j == 0 else C("amask")
                kcols = slice(j * 128, j * 128 + 256)
                for quad in range(4):
                    bk = [bank("aux"), bank("aux")]
                    for i4 in range(4):
                        h = quad * 4 + i4
                        hh = h % 2
                        hpair = h // 2
                        pb, tb = bk[hh]
                        co = (i4 // 2) * 256
                        P.I("pe", "matmul", [t_QT[j], t_KT[j], t_KT[j + 1]], [tb], inc=(i4 >= 2), out=pb[:, co:co + 256],
                            lhsT=QT[hh * 64:(hh + 1) * 64, hpair, j * 128:(j + 1) * 128], rhs=KTd[hh * 64:(hh + 1) * 64, h // 4, kcols], start=True, stop=True)
                    for hh in range(2):
                        pb, tb = bk[hh]
                        P.I("dve", "tensor_tensor", [tb, t_cst], [t_sc[s]], out=sc[s][:, quad * 4 + hh:quad * 4 + hh + 3:2, :],
                            in0=pb[:, 0:512].rearrange("p (h k) -> p h k", h=2), in1=bc(mk.unsqueeze(1), [128, 2, 256]), op=ALU.add)
                a_ = ast[s]
                ta = t_ast[s]
                P.I("dve", "tensor_reduce", [t_sc[s]], [ta], out=a_[:, 0, :], in_=sc[s], axis=AX.X, op=ALU.max)
                P.I("dve", "scalar_tensor_tensor", [ta, t_cb], [ta], out=a_[:, 1, :], in0=a_[:, 0, :], scalar=-0.125, in1=nsink, op0=ALU.mult, op1=ALU.min)
                P.I("dve", "memset", [], [ta], ap=a_[:, 2, :], constant=0.0)
                P.I("dve", "tensor_tensor", [ta, t_cst], [ta], out=a_[:, 3, :], in0=a_[:, 1, :], in1=C("sinks"), op=ALU.add)

            def att_B(j):
                s = j % 2
                a_ = ast[s]
                ta = t_ast[s]
                for h in range(16):
                    P.I("act", "activation", [t_sc[s], ta], [t_sc[s], ta], out=sc[s][:, h, :], in_=sc[s][:, h, :], func=AF.Exp, scale=0.125,
                        bias=a_[:, 1, h:h + 1], accum_out=a_[:, 2, h:h + 1])
                P.I("act", "activation", [ta], [ta], out=a_[:, 3, :], in_=a_[:, 3, :], func=AF.Exp)

            def att_C1(j):
                s = j % 2
                a_ = ast[s]
                ta = t_ast[s]
                P.I("dve", "tensor_tensor", [ta], [ta], out=a_[:, 4, :], in0=a_[:, 2, :], in1=a_[:, 3, :], op=ALU.add)
                P.I("dve", "reciprocal", [ta], [ta], out=a_[:, 5, :], in_=a_[:, 4, :])
                P.I("dve", "tensor_tensor", [t_sc[s], ta], [t_Pn], out=Pn, in0=sc[s], in1=bc(a_[:, 5, :].unsqueeze(2), [128, 16, 256]), op=ALU.mult)

            def att_C2(j):
                for q4 in range(4):
                    pb, tb = bank("aux")
                    pv = pb[:].bitcast(BF16)
                    for i in range(8):
                        idx = q4 * 8 + i
                        h, kb = idx // 2, idx % 2
                        P.I("pe", "transpose", [t_Pn, t_cb], [tb], inc=(i == 7), out=pv[:, i * 128:(i + 1) * 128], in_=Pn[:, h, kb * 128:(kb + 1) * 128], identity=identb)
                    P.I("act", "activation", [tb], [t_PT[q4]], out=PT[:, q4 * 8:(q4 + 1) * 8, :], in_=pv.rearrange("p (i q) -> p i q", i=8), func=AF.Copy)
                for half in range(2):
                    pb, tb = bank("acc")
                    for pr in range(4):
                        hp = half * 4 + pr
                        for hh in range(2):
                            h = hp * 2 + hh
                            kvh = h // 4
                            for kb in range(2):
                                P.I("pe", "matmul", [t_V[j + kb], t_PT[h // 4]], [tb], inc=(pr == 3 and hh == 1 and kb == 1),
                                    out=pb[hh * 64:(hh + 1) * 64, pr * 128:(pr + 1) * 128], lhsT=Vt[:, j + kb, kvh * 64:(kvh + 1) * 64],
                                    rhs=PT[:, h * 2 + kb, :], start=(kb == 0), stop=(kb == 1))
                    P.I("act", "activation", [tb], [t_attnT[j]], out=attnT[:, half * 4:(half + 1) * 4, j * 128:(j + 1) * 128],
                        in_=pb[:, 0:512].rearrange("p (a q) -> p a q", a=4), func=AF.Copy)

            att_A(0)
            att_B(0)
            if NT > 1:
                att_A(1)
            for j in range(NT):
                att_C1(j)
                if j + 2 < NT:
                    att_A(j + 2)
                if j + 1 < NT:
                    att_B(j + 1)
                att_C2(j)
            A.reset_hi(HI_S)
            P.barrier()
            stop_here("attn")

            XBX = A.lo_alloc([128, 16, TK], BF16)
            t_XBX = [[P.tile("xbxm%d_%d" % (b, g)) for g in range(NG)] for b in range(16)]
            rstd = A.lo_alloc([128, TK], F32)
            t_rstd = P.tiles("rstd", NG)
            XBB = A.alloc([128, 8, TK], BF16)
            t_XBB = [[P.tile("xbbm%d_%d" % (b, g)) for g in range(NG)] for b in range(8)]
            dtv = A.alloc([128, 8, NT, 32], F32)
            abf = A.alloc([128, NT, 32], BF16)
            t_dtv = P.tile("dtvm")
            dt_path(dtv, t_dtv, abf)
            xbc_path(XBB, t_XBB, True)
            stop_here("xbc2")
            ssd_chunks(XBB, t_XBB, dtv, t_dtv, False, abf)
            stop_here("ssd2")
            A.reset_hi(BASE)
            P.barrier()

            szb = [A.alloc([128, MW], F32) for _ in range(2)]
            vb = [A.alloc([128, MW], F32) for _ in range(2)]
            sqb = [A.alloc([128, MW], BF16) for _ in range(2)]
            t_szb = P.tiles("sz", 2)
            t_vb = P.tiles("vb", 2)
            t_sqb = P.tiles("sq", 2)
            ssq_banks = [bank("aux") for _ in range(NG)]
            units = [(pc, cb, tg) for pc in range(8) for cb in range(2) for tg in range(NG)]
            zw = {}

            def z_proj(n):
                pc, cb, tg = units[n]
                if (cb, tg) == (0, 0):
                    zw["w"] = acquire("w_in", 0, KC, OFF_Z + pc * 256, OFF_Z + (pc + 1) * 256)
                wz, t_wz = zw["w"]
                pb, tb = bank("acc")
                proj_ws(wz, t_wz, cb, KC, UT, tg_tiles(t_UT, tg), tg, pb, tb)
                zw[n] = (pb, tb)

            def z_silu(n):
                pb, tb = zw.pop(n)
                s = n % 2
                P.I("act", "activation", [tb], [t_szb[s]], out=szb[s], in_=pb[:, 0:MW], func=AF.Silu)

            def z_tail(n):
                pc, cb, tg = units[n]
                fb = pc * 2 + cb
                s = n % 2
                yv = XBX[:, fb, tg * MW:(tg + 1) * MW]
                P.I("dve", "tensor_tensor", [t_XBX[fb][tg], t_szb[s]], [t_vb[s]], out=vb[s], in0=yv, in1=szb[s], op=ALU.mult)
                P.I("act", "activation", [t_vb[s]], [t_sqb[s]], out=sqb[s], in_=vb[s], func=AF.Square)
                P.I("act", "activation", [t_vb[s], t_cst], [t_XBX[fb][tg]], out=yv, in_=vb[s], func=AF.Copy, scale=C("gssd", fb, fb + 1))
                pbs, tbs = ssq_banks[tg]
                P.I("pe", "matmul", [t_sqb[s], t_cb], [tbs], inc=(fb == 15), out=pbs[:, 0:MW], lhsT=onesb, rhs=sqb[s], start=(fb == 0), stop=(fb == 15))

            z_proj(0)
            z_silu(0)
            for n in range(len(units)):
                if n + 1 < len(units):
                    z_proj(n + 1)
                z_tail(n)
                if n + 1 < len(units):
                    z_silu(n + 1)
            for tg in range(NG):
                pbs, tbs = ssq_banks[tg]
                rv = rstd[:, tg * MW:(tg + 1) * MW]
                P.I("act", "activation", [tbs, t_cst], [t_rstd[tg]], out=rv, in_=pbs[:, 0:MW], func=AF.Sqrt, scale=1.0 / D, bias=C("epss"))
                P.I("dve", "reciprocal", [t_rstd[tg]], [t_rstd[tg]], out=rv, in_=rv)
            stop_here("zgate")
            A.reset_hi(BASE)
            P.barrier()

            MG = A.alloc([128, KC, TK], BF16)
            t_MG = P.tiles("MG", NT)
            HI_MG = A.mark()
            sg = [[A.alloc([128, MW], F32) for _ in range(NG)] for _ in range(2)]
            t_sg = [P.tiles("sg%d_" % c, NG) for c in range(2)]
            t1 = [[A.alloc([128, MW], F32) for _ in range(NG)] for _ in range(2)]
            t_t1 = [P.tiles("t1%d_" % c, NG) for c in range(2)]
            t2 = [A.alloc([128, MW], F32) for _ in range(2)]
            t_t2 = P.tiles("t2", 2)
            n = 0
            for pc in range(8):
                c0, c1 = pc * 256, (pc + 1) * 256
                wga, t_wga = acquire("w_in", 0, KC, OFF_GA + c0, OFF_GA + c1)
                for cb in range(2):
                    for tg in range(NG):
                        pb, tb = bank("acc")
                        proj_ws(wga, t_wga, cb, KC, UT, tg_tiles(t_UT, tg), tg, pb, tb)
                        P.I("act", "activation", [tb], [t_sg[cb][tg]], out=sg[cb][tg], in_=pb[:, 0:MW], func=AF.Sigmoid)
                wab, t_wab = acquire("w_attn_br", 0, 8, c0, c1)
                for cb in range(2):
                    for tg in range(NG):
                        pb, tb = bank("acc")
                        proj_ws(wab, t_wab, cb, 8, attnT, tg_tiles(t_attnT, tg), tg, pb, tb)
                        P.I("dve", "tensor_tensor", [tb, t_sg[cb][tg]], [t_t1[cb][tg]], out=t1[cb][tg], in0=pb[:, 0:MW], in1=sg[cb][tg], op=ALU.mult)
                wgs, t_wgs = acquire("w_in", 0, KC, OFF_GS + c0, OFF_GS + c1)
                for cb in range(2):
                    for tg in range(NG):
                        pb, tb = bank("acc")
                        proj_ws(wgs, t_wgs, cb, KC, UT, tg_tiles(t_UT, tg), tg, pb, tb)
                        P.I("act", "activation", [tb], [t_sg[cb][tg]], out=sg[cb][tg], in_=pb[:, 0:MW], func=AF.Sigmoid)
                        P.I("pool", "tensor_tensor", [t_sg[cb][tg], t_rstd[tg]], [t_sg[cb][tg]], out=sg[cb][tg], in0=sg[cb][tg], in1=rstd[:, tg * MW:(tg + 1) * MW], op=ALU.mult)
                wsb, t_wsb = acquire("w_ssd_br", 0, KC, c0, c1)
                for cb in range(2):
                    fb = pc * 2 + cb
                    for tg in range(NG):
                        pb, tb = bank("acc")
                        proj_ws(wsb, t_wsb, cb, KC, XBX, [t_XBX[k][tg] for k in range(16)], tg, pb, tb)
                        s = n % 2
                        n += 1
                        P.I("dve", "tensor_tensor", [tb, t_sg[cb][tg]], [t_t2[s]], out=t2[s], in0=pb[:, 0:MW], in1=sg[cb][tg], op=ALU.mult)
                        P.I("dve", "tensor_tensor", [t_t2[s], t_t1[cb][tg]], tg_tiles(t_MG, tg), out=MG[:, fb, tg * MW:(tg + 1) * MW], in0=t2[s], in1=t1[cb][tg], op=ALU.add)
            A.reset_lo(BASE)
            P.barrier()
            stop_here("merge")

            H = A.lo_alloc([128, NT, D], F32)
            t_H = P.tiles("H", NT, dma=True)
            for ti in range(NT):
                P.dma("sp", t_H[ti], DR, H[:, ti, :], x_d[ti * 128:(ti + 1) * 128, :])
            A.reset_hi(HI_MG)
            nctx = norm_ctx(3)
            LAG = 2
            for pc in range(8):
                c0, c1 = pc * 256, (pc + 1) * 256
                wo, t_wo = acquire("w_o", 0, KC, c0, c1)
                for ti in range(NT):
                    pb, tb = bank("acc")
                    for k in range(KC):
                        P.I("pe", "matmul", [t_wo, t_MG[ti]], [tb], inc=(k == KC - 1), out=pb[:, 0:256], lhsT=MG[:, k, ti * 128:(ti + 1) * 128],
                            rhs=wo[:, k, :], start=(k == 0), stop=(k == KC - 1))
                    P.I("dve", "tensor_tensor", [tb, t_H[ti]], [t_H[ti]], out=H[:, ti, c0:c1], in0=pb[:, 0:256], in1=H[:, ti, c0:c1], op=ALU.add)
                    if pc == 7:
                        norm_s1(nctx, H[:, ti, :], t_H[ti], ti)
                        if ti >= LAG:
                            norm_s2(nctx, "gffn", ti - LAG)
            for ti in range(max(0, NT - LAG), NT):
                norm_s2(nctx, "gffn", ti)
            A.reset_hi(BASE)
            P.barrier()
            stop_here("wo")

            def norm_from_H(gname):
                m = A.mark()
                ub = [A.alloc([128, D], BF16) for _ in range(2)]
                t_ub = P.tiles("ubh", 2)
                junk = A.alloc([128, D], BF16)
                t_junk = P.tile("junkh")
                norm_stage1(H[:, 0, :], t_H[0], ub[0], t_ub[0], junk, t_junk)
                for ti in range(NT):
                    if ti + 1 < NT:
                        s = (ti + 1) % 2
                        norm_stage1(H[:, ti + 1, :], t_H[ti + 1], ub[s], t_ub[s], junk, t_junk)
                    norm_stage2(gname, ti, ub[ti % 2], t_ub[ti % 2])
                A.reset(m)
                P.barrier()

            ACTT = A.alloc([128, 16, TK], BF16)
            t_ACT = [[P.tile("act%d_%d" % (c, g)) for g in range(NG)] for c in range(16)]
            sgf = [[A.alloc([128, MW], F32) for _ in range(NG)] for _ in range(2)]
            t_sgf = [P.tiles("sgf%d_" % c, NG) for c in range(2)]
            NCH = FF // 128
            pctx = norm_ctx(2)
            LAGP = 1
            groups = []
            c_ = 0
            while c_ < NCH:
                g_ = min(8, NCH - c_)
                groups.append((c_, g_))
                c_ += g_

            def ffn_gateup(gi):
                ch0, gch = groups[gi]
                ring = (gi % 2) * 8
                for pp in range(gch // 2):
                    c0 = (ch0 + pp * 2) * 128
                    wgt, t_wgt = acquire("w_gate", 0, KC, c0, c0 + 256)
                    for cb in range(2):
                        for tg in range(NG):
                            pb, tb = bank("acc")
                            proj_ws(wgt, t_wgt, cb, KC, UT, tg_tiles(t_UT, tg), tg, pb, tb)
                            P.I("act", "activation", [tb], [t_sgf[cb][tg]], out=sgf[cb][tg], in_=pb[:, 0:MW], func=AF.Silu)
                    wup, t_wup = acquire("w_up", 0, KC, c0, c0 + 256)
                    for cb in range(2):
                        slot = ring + pp * 2 + cb
                        for tg in range(NG):
                            pb, tb = bank("acc")
                            proj_ws(wup, t_wup, cb, KC, UT, tg_tiles(t_UT, tg), tg, pb, tb)
                            P.I("dve", "tensor_tensor", [tb, t_sgf[cb][tg]], [t_ACT[slot][tg]], out=ACTT[:, slot, tg * MW:(tg + 1) * MW], in0=pb[:, 0:MW], in1=sgf[cb][tg], op=ALU.mult)

            def ffn_down(gi):
                ch0, gch = groups[gi]
                ring = (gi % 2) * 8
                last_grp = gi == len(groups) - 1
                for cp in range(4):
                    wdn, t_wdn = acquire("w_down", ch0, ch0 + gch, cp * 512, (cp + 1) * 512)
                    for ti in range(NT):
                        tg = ti // TPG
                        pb, tb = bank("acc")
                        for k in range(gch):
                            P.I("pe", "matmul", [t_wdn, t_ACT[ring + k][tg]], [tb], inc=(k == gch - 1), out=pb[:, 0:512], lhsT=ACTT[:, ring + k, ti * 128:(ti + 1) * 128],
                                rhs=wdn[:, k, :], start=(k == 0), stop=(k == gch - 1))
                        P.I("dve", "tensor_tensor", [tb, t_H[ti]], [t_H[ti]], out=H[:, ti, cp * 512:(cp + 1) * 512], in0=pb[:, 0:512], in1=H[:, ti, cp * 512:(cp + 1) * 512], op=ALU.add)
                        if last_grp and cp == 3:
                            norm_s1(pctx, H[:, ti, :], t_H[ti], ti)
                            if ti >= LAGP:
                                norm_s2(pctx, "gple", ti - LAGP)
                if last_grp:
                    for ti in range(max(0, NT - LAGP), NT):
                        norm_s2(pctx, "gple", ti)

            ffn_gateup(0)
            for gi in range(len(groups)):
                if gi + 1 < len(groups):
                    ffn_gateup(gi + 1)
                ffn_down(gi)
            A.reset_hi(BASE)
            P.barrier()
            stop_here("ffn")

            pT = A.alloc([128, 2, TK], BF16)
            t_pT = P.tiles("pT", NT)
            ps = [A.alloc([128, PLE], F32) for _ in range(2)]
            t_ps = P.tiles("ps", 2, dma=True)
            pbf = [A.alloc([128, PLE], BF16) for _ in range(2)]
            t_pbf = P.tiles("pbf", 2)
            for ti in range(NT):
                s = ti % 2
                P.dma("sp", t_ps[s], DR, ps[s], p_d[ti * 128:(ti + 1) * 128, :])
                P.I("dve", "tensor_copy", [t_ps[s]], [t_pbf[s]], out=pbf[s], in_=ps[s])
                pb, tb = bank("aux")
                pv = pb[:].bitcast(BF16)
                for k in range(2):
                    P.I("pe", "transpose", [t_pbf[s], t_cb], [tb], inc=(k == 1), out=pv[:, k * 128:(k + 1) * 128], in_=pbf[s][:, k * 128:(k + 1) * 128], identity=identb)
                P.I("dve", "tensor_copy", [tb], [t_pT[ti]], out=pT[:, :, ti * 128:(ti + 1) * 128], in_=pv[:, 0:256].rearrange("p (k t) -> p k t", k=2))
            sgp = [A.alloc([128, 256], F32) for _ in range(NT)]
            t_sgp = P.tiles("sgp", NT)
            tp = [A.alloc([128, 256], F32) for _ in range(2)]
            t_tp = P.tiles("tp", 2)
            n = 0
            for pc in range(8):
                c0, c1 = pc * 256, (pc + 1) * 256
                wpg, t_wpg = acquire("w_ple_gate", 0, KC, c0, c1)
                for ti in range(NT):
                    pb, tb = bank("acc")
                    for k in range(KC):
                        P.I("pe", "matmul", [t_wpg, t_UT[ti]], [tb], inc=(k == KC - 1), out=pb[:, 0:256], lhsT=UT[:, k, ti * 128:(ti + 1) * 128],
                            rhs=wpg[:, k, :], start=(k == 0), stop=(k == KC - 1))
                    P.I("act", "activation", [tb], [t_sgp[ti]], out=sgp[ti], in_=pb[:, 0:256], func=AF.Sigmoid)
                wpp, t_wpp = acquire("w_ple_proj", 0, 2, c0, c1)
                for ti in range(NT):
                    pb, tb = bank("acc")
                    for k in range(2):
                        P.I("pe", "matmul", [t_wpp, t_pT[ti]], [tb], inc=(k == 1), out=pb[:, 0:256], lhsT=pT[:, k, ti * 128:(ti + 1) * 128],
                            rhs=wpp[:, k, :], start=(k == 0), stop=(k == 1))
                    s = n % 2
                    n += 1
                    P.I("dve", "tensor_tensor", [tb, t_sgp[ti]], [t_tp[s]], out=tp[s], in0=pb[:, 0:256], in1=sgp[ti], op=ALU.mult)
                    P.I("dve", "tensor_tensor", [t_tp[s], t_H[ti]], [t_H[ti]], out=H[:, ti, c0:c1], in0=tp[s], in1=H[:, ti, c0:c1], op=ALU.add)
            A.reset_hi(BASE)
            P.barrier()
            stop_here("ple")

            gfb = A.alloc([128, D], F32)
            t_gfb = P.tile("gfb", dma=True)
            P.dma("sp", t_gfb, DR, gfb, gfin_d.partition_broadcast(128))
            ost = [A.alloc([128, D], F32) for _ in range(2)]
            t_ost = P.tiles("ost", 2, dma=True)
            junk = A.alloc([128, D], BF16)
            t_junk = P.tile("junkf")
            outs = []
            for ti in range(NT):
                s = ti % 2
                ss, t_ss = stat_slot()
                P.I("dve", "memset", [], [t_ss], ap=ss[:, 0:1], constant=0.0)
                P.I("act", "activation", [t_H[ti], t_ss], [t_junk, t_ss], out=junk, in_=H[:, ti, :], func=AF.Square, accum_out=ss[:, 0:1])
                P.I("act", "activation", [t_ss, t_cst], [t_ss], out=ss[:, 1:2], in_=ss[:, 0:1], func=AF.Sqrt, scale=1.0 / D, bias=C("epsn"))
                P.I("dve", "reciprocal", [t_ss], [t_ss], out=ss[:, 2:3], in_=ss[:, 1:2])
                P.I("dve", "scalar_tensor_tensor", [t_H[ti], t_ss, t_gfb], [t_ost[s]], out=ost[s], in0=H[:, ti, :], scalar=ss[:, 2:3], in1=gfb, op0=ALU.mult, op1=ALU.mult)
                outs.append(P.dma("sp", DR, t_ost[s], y_d[ti * 128:(ti + 1) * 128, :], ost[s], sem_t=t_ost[s]))
            for ev in outs[-2:]:
                P.wait_on("sp", ev)

        except _Stop:
            pass
        if plan is not None:
            P.emit()
    build.phases = PH
    return nc, ws["req"]


def _pack_cst(inp, hf, NT):
    c = np.zeros((128, CST_W), np.float32)

    def put(name, arr):
        o, w = CST[name]
        c[:, o:o + w] = arr

    k = np.arange(128)
    put("ident", np.eye(128, dtype=np.float32))
    put("tri", (k[:, None] <= k[None, :]).astype(np.float32))
    put("ustr", (k[:, None] > k[None, :]).astype(np.float32))
    put("ones", np.ones((128, 128), np.float32))
    rot = np.zeros((128, 128), np.float32)
    for pp in range(128):
        d = pp % 64
        if d < 32:
            rot[pp + 32, pp] = -1.0
        else:
            rot[pp - 32, pp] = 1.0
    put("rotm", rot)
    am = np.full((128, 256), MASKV, np.float32)
    q = k[:, None]
    kk = k[None, :]
    am[:, 0:128][kk > q] = 0.0
    am[:, 128:256][kk <= q] = 0.0
    put("amask", am)
    am0 = am.copy()
    if hf == 0:
        am0[:, 0:128] = MASKV
    put("amask0", am0)
    half = 32
    invf = (10000.0 ** (-np.arange(half, dtype=np.float32) * 2.0 / 64)).astype(np.float32)
    put("invf", invf[k % 32][:, None])
    put("flag", np.full((128, 1), float(hf), np.float32))
    put("epsn", np.full((128, 1), EPS, np.float32))
    put("epss", np.full((128, 1), SSM_EPS, np.float32))
    put("one", np.ones((128, 1), np.float32))
    col = lambda v: np.ascontiguousarray(np.asarray(v, np.float32).reshape(-1, 128).T)
    put("gmix", col(inp["g_mix"][0]))
    put("gffn", col(inp["g_ffn"][0]))
    put("gple", col(inp["g_ple"][0]))
    put("gssd", col(inp["g_ssd"][0]))
    cw = np.asarray(inp["conv_w"][0], np.float32)
    put("convw", np.ascontiguousarray(cw.reshape(4, 24, 128).transpose(2, 1, 0)).reshape(128, 96))
    put("convb", col(inp["conv_b"][0]))
    ds = np.asarray(inp["d_skip"][0], np.float32)
    put("dcol", np.ascontiguousarray(np.repeat(ds, 64).reshape(16, 128).T))
    put("dtb", np.broadcast_to(np.asarray(inp["dt_bias"][0], np.float32)[None, :], (128, 32)))
    put("alog", np.broadcast_to(np.asarray(inp["a_log"][0], np.float32)[None, :], (128, 32)))
    put("sinks", np.broadcast_to(np.asarray(inp["sinks"][0], np.float32)[None, :], (128, 16)))
    return c


_CACHE = {}


def _get_nc(NT):
    if NT not in _CACHE:
        _, req = build(NT, plan=None)
        nc, _ = build(NT, plan=req)
        _CACHE[NT] = nc
    return _CACHE[NT]


def make_in_maps(inp, NT):
    x = np.asarray(inp["x"], np.float32)
    B, S, _ = x.shape
    TK = NT * 128
    assert S == 2 * TK
    pos = np.asarray(inp["positions"]).astype(np.int32)
    p = np.asarray(inp["p"], np.float32)[0]
    shared = {k: np.ascontiguousarray(np.asarray(inp[k], np.float32)[0]) for k in WSPEC}
    shared["g_final"] = np.ascontiguousarray(np.asarray(inp["g_final"], np.float32))
    maps = []
    for b in range(B):
        for hf in range(2):
            t0 = hf * TK
            m = dict(shared)
            m["x"] = np.ascontiguousarray(x[b, t0:t0 + TK])
            m["xprev"] = np.ascontiguousarray(x[b, 0:TK]) if hf == 1 else np.zeros((TK, D), np.float32)
            pe = np.zeros(128 + TK, np.int32)
            pe[128:] = pos[b, t0:t0 + TK]
            if hf == 1:
                pe[:128] = pos[b, t0 - 128:t0]
            m["pos"] = pe
            m["p"] = np.ascontiguousarray(p[b, t0:t0 + TK])
            m["cst"] = _pack_cst(inp, hf, NT)
            maps.append(m)
    return maps


def kernel(**inputs):
    x = np.asarray(inputs["x"])
    B, S, _ = x.shape
    NT = S // 256
    nc = _get_nc(NT)
    maps = make_in_maps(inputs, NT)
    res = run_bass_kernel_spmd(nc, maps, core_ids=list(range(len(maps))))
    out = np.zeros((B, S, D), np.float32)
    TK = NT * 128
    i = 0
    for b in range(B):
        for hf in range(2):
            out[b, hf * TK:(hf + 1) * TK] = res.results[i]["y"]
            i += 1
    return out
```

```python
import math
import numpy as np
from contextlib import ExitStack
import concourse.bass as bass
import concourse.mybir as mybir
from concourse.bass_utils import run_bass_kernel_spmd

F32 = mybir.dt.float32
BF16 = mybir.dt.bfloat16
I32 = mybir.dt.int32
ALU = mybir.AluOpType
AF = mybir.ActivationFunctionType
AX = mybir.AxisListType

D = 2048
KC = 16
Q_DIM = 1024
FF = 5632
PLE = 256
NHEAD_S = 32
OFF_Q, OFF_K, OFF_V, OFF_Z, OFF_XBC, OFF_DT, OFF_GA, OFF_GS, IN_DIM = 0, 1024, 1280, 1536, 3584, 6656, 6688, 8736, 10784
EPS = 1e-6
SSM_EPS = 1e-5
MASKV = -240000.0

ENGINES = ("pe", "act", "dve", "pool", "sp")


class T:
    __slots__ = ("name", "last_w", "readers", "dsem")

    def __init__(self, name):
        self.name = name
        self.last_w = None
        self.readers = []
        self.dsem = None


class Prog:
    def __init__(self, nc, stack):
        self.nc = nc
        self.stack = stack
        self.ops = {e: [] for e in ENGINES}
        self.sems = {}
        self.semval = {}
        self.waited = {e: {} for e in ENGINES}
        for e in ENGINES:
            self._mksem("E_" + e)
        self.n_dsem = 0
        self.epoch = []
        self.dram = T("dram")

    def _mksem(self, key):
        h = self.stack.enter_context(self.nc.semaphore(key))
        self.sems[key] = h
        self.semval[key] = 0
        return key

    def tile(self, name, dma=False):
        t = T(name)
        t.readers = list(self.epoch)
        if dma:
            t.dsem = self._mksem("D%d" % self.n_dsem)
            self.n_dsem += 1
        return t

    def tiles(self, name, n, dma=False):
        return [self.tile("%s%d" % (name, i), dma) for i in range(n)]

    def _collect_waits(self, eng, reads, writes):
        need = {}
        me = "E_" + eng
        pe = eng == "pe"

        def add(ev, skip_same):
            if ev is None:
                return
            k, v = ev
            if skip_same and k == me:
                return
            if need.get(k, 0) < v:
                need[k] = v

        for t in reads:
            add(t.last_w, pe)
        for t in writes:
            add(t.last_w, pe)
            for r in t.readers:
                add(r, pe)
        out = []
        w = self.waited[eng]
        for k, v in need.items():
            if w.get(k, 0) < v:
                w[k] = v
                out.append((k, v))
        return out

    def op(self, eng, fn, reads=(), writes=(), inc=True):
        waits = self._collect_waits(eng, reads, writes)
        key = "E_" + eng
        if inc:
            self.semval[key] += 1
            ev = (key, self.semval[key])
        else:
            ev = (key, self.semval[key] + 1)
        for t in reads:
            if t is not self.dram:
                t.readers.append(ev)
        for t in writes:
            t.last_w = ev
            t.readers = []
        self.ops[eng].append((waits, fn, key if inc else None))
        return ev

    def I(self, eng, name, reads, writes, inc=True, **kw):
        return self.op(eng, lambda e, name=name, kw=kw: getattr(e, name)(**kw), reads, writes, inc)

    def dma(self, eng, out_t, in_t, out_ap, in_ap, sem_t=None):
        sem_t = sem_t or (out_t if out_t.dsem else in_t)
        waits = self._collect_waits(eng, [in_t], [out_t])
        k = sem_t.dsem
        self.semval[k] += 16
        ev = (k, self.semval[k])
        if in_t is not self.dram:
            in_t.readers.append(ev)
        if out_t is not self.dram:
            out_t.last_w = ev
            out_t.readers = []

        def fn(e, out_ap=out_ap, in_ap=in_ap):
            return e.dma_start(out=out_ap, in_=in_ap)
        self.ops[eng].append((waits, fn, ("DMA", k)))
        return ev

    def wait_on(self, eng, ev):
        k, v = ev
        if self.waited[eng].get(k, 0) < v:
            self.waited[eng][k] = v
            self.ops[eng].append(([(k, v)], None, None))

    def barrier(self, hard=False):
        snap = dict(self.semval)
        self.epoch = [(k, v) for k, v in snap.items() if v > 0]
        if hard:
            for e in ENGINES:
                for k, v in snap.items():
                    if v > 0 and k != "E_" + e:
                        self.wait_on(e, (k, v))

    def emit(self):
        nc = self.nc
        with nc.Block() as block:
            def run(engname, e):
                for waits, fn, inc in self.ops[engname]:
                    for k, v in waits:
                        e.wait_ge(self.sems[k], v)
                    if fn is None:
                        continue
                    ins = fn(e)
                    if inc is None:
                        continue
                    if isinstance(inc, tuple):
                        ins.then_inc(self.sems[inc[1]], 16)
                    else:
                        ins.then_inc(self.sems[inc], 1)

            @block.tensor
            def _(e):
                run("pe", e)

            @block.scalar
            def _(e):
                run("act", e)

            @block.vector
            def _(e):
                run("dve", e)

            @block.gpsimd
            def _(e):
                run("pool", e)

            @block.sync
            def _(e):
                run("sp", e)


class Arena:
    def __init__(self, ap, nwords):
        self.ap = ap
        self.n = nwords
        self.lo = 0
        self.hi = nwords

    def _view(self, off, words, shape, dtype, n):
        v = self.ap[:, off:off + words]
        if dtype != F32:
            v = v.bitcast(dtype)
        if v.shape[1] != n:
            v = v[:, 0:n]
        if len(shape) == 3:
            v = v.rearrange("p (a b) -> p a b", a=shape[1])
        elif len(shape) == 4:
            v = v.rearrange("p (a b c) -> p a b c", a=shape[1], b=shape[2])
        return v

    def alloc(self, shape, dtype, hi=True):
        n = 1
        for s in shape[1:]:
            n *= s
        esz = 4 if dtype in (F32, I32) else 2
        words = (n * esz + 3) // 4
        words = (words + 7) // 8 * 8
        assert self.lo + words <= self.hi, ("arena overflow", self.lo, self.hi, words)
        if hi:
            self.hi -= words
            off = self.hi
        else:
            off = self.lo
            self.lo += words
        self.last = (off, words)
        return self._view(off, words, shape, dtype, n)

    def lo_alloc(self, shape, dtype):
        return self.alloc(shape, dtype, hi=False)

    def mark(self):
        return (self.lo, self.hi)

    def reset(self, m):
        self.lo, self.hi = m

    def reset_hi(self, m):
        self.hi = m[1]

    def reset_lo(self, m):
        self.lo = m[0]


def bc(ap, shape):
    return ap.broadcast_to(list(shape))


def _cst_layout():
    names = [("ident", 128), ("tri", 128), ("ustr", 128), ("ones", 128), ("rotm", 128), ("amask", 256),
             ("amask0", 256), ("invf", 1), ("flag", 1), ("gmix", 16), ("gffn", 16), ("gple", 16), ("gssd", 16),
             ("convw", 96), ("convb", 24), ("dcol", 16), ("dtb", 32), ("alog", 32), ("sinks", 16), ("epsn", 1), ("epss", 1), ("one", 1)]
    off = {}
    o = 0
    for n, w in names:
        off[n] = (o, w)
        o += w
    return off, o


CST, CST_W = _cst_layout()


WSPEC = {"w_in": (D, IN_DIM), "w_attn_br": (Q_DIM, D), "w_ssd_br": (D, D), "w_o": (D, D), "w_gate": (D, FF),
         "w_up": (D, FF), "w_down": (FF, D), "w_ple_gate": (D, D), "w_ple_proj": (PLE, D)}


def build(NT, plan=None, stop_after=None):
    TK = NT * 128
    MW = min(512, TK)
    NG = TK // MW
    TPG = MW // 128
    nc = bass.Bass("TRN2", target_bir_lowering=False)

    def din(name, shape, dt=F32):
        return nc.dram_tensor(name, list(shape), dt, kind="ExternalInput").ap()

    x_d = din("x", [TK, D])
    xp_d = din("xprev", [TK, D])
    p_d = din("p", [TK, PLE])
    pos_d = din("pos", [128 + TK], I32)
    cst_d = din("cst", [128, CST_W])
    gfin_d = din("g_final", [D])
    WD = {k: din(k, list(v)) for k, v in WSPEC.items()}
    y_d = nc.dram_tensor("y", [TK, D], F32, kind="ExternalOutput").ap()

    def wview(spec):
        name, k0, k1, c0, c1 = spec
        return WD[name].rearrange("(k p) n -> p k n", p=128)[:, k0:k1, c0:c1]

    st = ExitStack()
    with st:
        P = Prog(nc, st)
        AW = 53000
        arena_t = st.enter_context(nc.sbuf_tensor("arena", [128, AW], F32))
        A = Arena(arena_t[:], AW)
        pbank = [st.enter_context(nc.psum_tensor("pb%d" % i, [128, 512], F32)) for i in range(8)]
        t_bank = P.tiles("bank", 8)
        rr = {"acc": 0, "aux": 0}

        def bank(pool):
            i = rr[pool]
            rr[pool] = (i + 1) % 4
            j = i + (0 if pool == "acc" else 4)
            return pbank[j], t_bank[j]

        DR = P.dram

        cst = A.lo_alloc([128, CST_W], F32)
        t_cst = P.tile("cst", dma=True)
        P.dma("sp", t_cst, DR, cst, cst_d)

        def C(name, a=None, b=None):
            o, w = CST[name]
            if a is None:
                return cst[:, o:o + w]
            return cst[:, o + a:o + b]

        identb = A.lo_alloc([128, 128], BF16)
        rotmb = A.lo_alloc([128, 128], BF16)
        t_cb = P.tile("cstb")
        P.I("dve", "tensor_copy", [t_cst], [t_cb], out=identb, in_=C("ident"))
        P.I("dve", "tensor_copy", [t_cst], [t_cb], out=rotmb, in_=C("rotm"))
        ustrb = A.lo_alloc([128, 128], BF16)
        onesb = A.lo_alloc([128, 128], BF16)
        trib = A.lo_alloc([128, 128], BF16)
        P.I("dve", "tensor_copy", [t_cst], [t_cb], out=ustrb, in_=C("ustr"))
        P.I("dve", "tensor_copy", [t_cst], [t_cb], out=onesb, in_=C("ones"))
        P.I("dve", "tensor_copy", [t_cst], [t_cb], out=trib, in_=C("tri"))
        aneg = A.lo_alloc([128, 32], F32)
        nsink = A.lo_alloc([128, 16], F32)
        P.I("act", "activation", [t_cst], [t_cb], out=aneg, in_=C("alog"), func=AF.Exp)
        P.I("dve", "tensor_scalar", [t_cb], [t_cb], out=aneg, in0=aneg, scalar1=-1.0, scalar2=None, op0=ALU.mult)
        P.I("dve", "tensor_scalar", [t_cst], [t_cb], out=nsink, in0=C("sinks"), scalar1=-1.0, scalar2=None, op0=ALU.mult)

        ws_stage = [A.lo_alloc([128, 4096], F32) for _ in range(2)]
        ws_bf = [A.lo_alloc([128, 4096], BF16) for _ in range(2)]
        t_stage = P.tiles("wstage", 2, dma=True)
        t_wbf = P.tiles("wbf", 2)
        ws = {"i": 0, "dma": 0, "cast": 0, "req": []}

        def ws_advance(i):
            n = len(plan)
            while True:
                progressed = False
                j = ws["dma"]
                if j < n and j <= i + 2 and (j < 2 or ws["cast"] > j - 2):
                    v = wview(plan[j])
                    kc, ncol = v.shape[1], v.shape[2]
                    s = j % 2
                    P.dma("sp", t_stage[s], DR, ws_stage[s][:, 0:kc * ncol].rearrange("p (k c) -> p k c", k=kc), v)
                    ws["dma"] += 1
                    progressed = True
                j = ws["cast"]
                if j < n and j <= i + 1 and j < ws["dma"]:
                    _, k0, k1, c0, c1 = plan[j]
                    sz = (k1 - k0) * (c1 - c0)
                    s = j % 2
                    P.I("act", "activation", [t_stage[s]], [t_wbf[s]], out=ws_bf[s][:, 0:sz], in_=ws_stage[s][:, 0:sz], func=AF.Copy)
                    ws["cast"] += 1
                    progressed = True
                if not progressed:
                    break

        def acquire(name, k0, k1, c0, c1):
            spec = (name, k0, k1, c0, c1)
            kc, ncol = k1 - k0, c1 - c0
            assert kc * ncol <= 4096
            i = ws["i"]
            ws["i"] += 1
            if plan is None:
                ws["req"].append(spec)
            else:
                assert plan[i] == spec, (i, plan[i], spec)
                ws_advance(i)
            s = i % 2
            return ws_bf[s][:, 0:kc * ncol].rearrange("p (k c) -> p k c", k=kc), t_wbf[s]

        UT = A.lo_alloc([128, KC, TK], BF16)
        t_UT = P.tiles("UT", NT)
        stat = A.lo_alloc([128, 64], F32)
        t_stat = P.tiles("stat", 16)
        stat_i = [0]

        def stat_slot():
            i = stat_i[0]
            stat_i[0] = (i + 1) % 16
            return stat[:, 4 * i:4 * i + 4], t_stat[i]

        BASE = A.mark()

        class _Stop(Exception):
            pass

        PH = []

        def stop_here(tag):
            PH.append((tag, sum(1 for o in P.ops["pe"] if o[1] is not None)))
            if stop_after == tag:
                P.barrier(hard=True)
                raise _Stop()

        try:
            def norm_stage1(src, t_src, ub, t_ub, junk, t_junk):
                ss, t_ss = stat_slot()
                P.I("dve", "memset", [], [t_ss], ap=ss[:, 0:1], constant=0.0)
                P.I("act", "activation", [t_src, t_ss], [t_junk, t_ss], out=junk, in_=src, func=AF.Square, accum_out=ss[:, 0:1])
                P.I("act", "activation", [t_ss], [t_ss], out=ss[:, 1:2], in_=ss[:, 0:1], func=AF.Sqrt, scale=1.0 / D, bias=C("epsn"))
                P.I("dve", "reciprocal", [t_ss], [t_ss], out=ss[:, 2:3], in_=ss[:, 1:2])
                P.I("act", "activation", [t_src, t_ss], [t_ub], out=ub, in_=src, func=AF.Copy, scale=ss[:, 2:3])

            def norm_stage2(gname, ti, ub, t_ub):
                for hb in range(2):
                    pb, tb = bank("aux")
                    pv = pb[:].bitcast(BF16)
                    for k in range(8):
                        kc = hb * 8 + k
                        P.I("pe", "transpose", [t_ub, t_cb], [tb], inc=(k == 7), out=pv[:, k * 128:(k + 1) * 128],
                            in_=ub[:, kc * 128:(kc + 1) * 128], identity=identb)
                    P.I("dve", "tensor_tensor", [tb, t_cst], [t_UT[ti]], out=UT[:, hb * 8:(hb + 1) * 8, ti * 128:(ti + 1) * 128],
                        in0=pv.rearrange("p (k t) -> p k t", k=8),
                        in1=bc(C(gname, hb * 8, hb * 8 + 8).unsqueeze(2), [128, 8, 128]), op=ALU.mult)

            def norm_ctx(nslots):
                ub = [A.alloc([128, D], BF16) for _ in range(nslots)]
                t_ub = P.tiles("ubx", nslots)
                junk = A.alloc([128, D], BF16)
                t_junk = P.tile("junkx")
                return {"ub": ub, "t_ub": t_ub, "junk": junk, "t_junk": t_junk, "n": nslots}

            def norm_s1(ctx, src, t_src, ti):
                s = ti % ctx["n"]
                norm_stage1(src, t_src, ctx["ub"][s], ctx["t_ub"][s], ctx["junk"], ctx["t_junk"])

            def norm_s2(ctx, gname, ti):
                s = ti % ctx["n"]
                norm_stage2(gname, ti, ctx["ub"][s], ctx["t_ub"][s])

            def proj_ws(wb, t_wb, cb, nk, srcT, t_src_list, tg, pb, tb):
                for k in range(nk):
                    P.I("pe", "matmul", [t_wb] + t_src_list, [tb], inc=(k == nk - 1), out=pb[:, 0:MW],
                        lhsT=wb[:, k, cb * 128:(cb + 1) * 128], rhs=srcT[:, k, tg * MW:(tg + 1) * MW],
                        start=(k == 0), stop=(k == nk - 1))

            def tg_tiles(tlist, tg):
                return tlist[tg * TPG:(tg + 1) * TPG]

            def norm_from_dram(x_dram, gname):
                m = A.mark()
                xs = [A.alloc([128, D], F32) for _ in range(2)]
                t_xs = P.tiles("xs", 2, dma=True)
                ub = [A.alloc([128, D], BF16) for _ in range(2)]
                t_ub = P.tiles("ub", 2)
                junk = A.alloc([128, D], BF16)
                t_junk = P.tile("junk")

                def s1(ti):
                    s = ti % 2
                    P.dma("sp", t_xs[s], DR, xs[s], x_dram[ti * 128:(ti + 1) * 128, :])
                    norm_stage1(xs[s], t_xs[s], ub[s], t_ub[s], junk, t_junk)
                s1(0)
                for ti in range(NT):
                    if ti + 1 < NT:
                        s1(ti + 1)
                    norm_stage2(gname, ti, ub[ti % 2], t_ub[ti % 2])
                A.reset(m)
                P.barrier()

            def trig_tables(c0, n, cosT, sinT, t_trig):
                m = A.mark()
                posi = A.alloc([128, n], I32)
                ang = A.alloc([128, n], F32)
                kq = A.alloc([128, n], F32)
                ki = A.alloc([128, n], I32)
                t_pos = P.tile("pos", dma=True)
                t_tmp = P.tile("trigtmp")
                P.dma("sp", t_pos, DR, posi, pos_d[c0:c0 + n].partition_broadcast(128))
                P.I("dve", "tensor_copy", [t_pos], [t_tmp], out=ang, in_=posi)
                P.I("dve", "tensor_scalar", [t_tmp, t_cst], [t_tmp], out=ang, in0=ang, scalar1=C("invf"), scalar2=None, op0=ALU.mult)
                C1 = 6.28125
                C2 = 2.0 * math.pi - C1
                for dst, shift in ((sinT, 0.0), (cosT, math.pi / 2.0)):
                    wr = [t_tmp, t_trig]
                    P.I("dve", "tensor_scalar", [t_tmp], [t_tmp], out=kq, in0=ang, scalar1=shift, scalar2=1.0 / (2.0 * math.pi), op0=ALU.add, op1=ALU.mult)
                    P.I("dve", "tensor_copy", [t_tmp], [t_tmp], out=ki, in_=kq)
                    P.I("dve", "tensor_copy", [t_tmp], [t_tmp], out=kq, in_=ki)
                    P.I("dve", "scalar_tensor_tensor", [t_tmp], wr, out=dst, in0=kq, scalar=-C1, in1=ang, op0=ALU.mult, op1=ALU.add)
                    P.I("dve", "scalar_tensor_tensor", [t_tmp, t_trig], wr, out=dst, in0=kq, scalar=-C2, in1=dst, op0=ALU.mult, op1=ALU.add)
                    if shift != 0.0:
                        P.I("dve", "tensor_scalar", [t_trig], wr, out=dst, in0=dst, scalar1=shift, scalar2=None, op0=ALU.add)
                    P.I("dve", "tensor_scalar", [t_trig], wr, out=kq, in0=dst, scalar1=math.pi, scalar2=-2.0 * math.pi, op0=ALU.is_gt, op1=ALU.mult)
                    P.I("dve", "tensor_tensor", [t_tmp, t_trig], wr, out=dst, in0=dst, in1=kq, op=ALU.add)
                    P.I("dve", "tensor_scalar", [t_trig], wr, out=kq, in0=dst, scalar1=-math.pi, scalar2=2.0 * math.pi, op0=ALU.is_lt, op1=ALU.mult)
                    P.I("dve", "tensor_tensor", [t_tmp, t_trig], wr, out=dst, in0=dst, in1=kq, op=ALU.add)
                    P.I("dve", "tensor_scalar", [t_trig], wr, out=dst, in0=dst, scalar1=math.pi, scalar2=-math.pi, op0=ALU.min, op1=ALU.max)
                    P.I("act", "activation", [t_trig], wr, out=dst, in_=dst, func=AF.Sin)
                A.reset(m)

            def rope_evac(pb, tb, dst, t_dst, ncols, cosv, sinv, t_trig, scr, t_scr):
                qb, r1, r2 = scr
                P.I("act", "activation", [tb], [t_scr], out=qb[:, 0:ncols], in_=pb[:, 0:ncols], func=AF.Copy)
                pb2, tb2 = bank("aux")
                P.I("pe", "matmul", [t_scr, t_cb], [tb2], out=pb2[:, 0:ncols], lhsT=rotmb, rhs=qb[:, 0:ncols], start=True, stop=True)
                P.I("dve", "tensor_tensor", [tb, t_trig], [t_scr], out=r1[:, 0:ncols], in0=pb[:, 0:ncols], in1=cosv, op=ALU.mult)
                P.I("dve", "tensor_tensor", [tb2, t_trig], [t_scr], out=r2[:, 0:ncols], in0=pb2[:, 0:ncols], in1=sinv, op=ALU.mult)
                P.I("dve", "tensor_tensor", [t_scr], t_dst, out=dst, in0=r1[:, 0:ncols], in1=r2[:, 0:ncols], op=ALU.add)

            attnT = A.lo_alloc([128, 8, TK], BF16)
            t_attnT = P.tiles("attnT", NT)
            LO_A = A.mark()
            XBX = A.lo_alloc([128, 16, TK], BF16)
            t_XBX = [[P.tile("xbx%d_%d" % (b, g)) for g in range(NG)] for b in range(16)]
            S32 = A.alloc([128, 4, 512], F32)
            Sbf = A.alloc([128, 4, 512], BF16)
            t_S = P.tiles("S", 4)
            t_Sbf = P.tiles("Sbf", 4)
            halo = A.alloc([128, 24, 4], F32)
            t_halo = P.tiles("halo", 24)
            HI_S = A.mark()
            KTd = A.alloc([128, 4, 128 + TK], BF16)
            _ktd_off = A.last
            t_KT = P.tiles("KT", NT + 1)
            Vt = A.alloc([128, NT + 1, 256], BF16)
            _vt_off = A.last
            t_V = P.tiles("V", NT + 1)
            HI_KV = A.mark()
            DBG = {}

            for g in range(4):
                P.I("dve", "memset", [], [t_S[g]], ap=S32[:, g, :], constant=0.0)
                P.I("dve", "memset", [], [t_Sbf[g]], ap=Sbf[:, g, :], constant=0.0)
            for b in range(24):
                P.I("pool", "memset", [], [t_halo[b]], ap=halo[:, b, :], constant=0.0)

            def k_proj(tiles_mode, cosT, sinT, t_trig, scr, t_scr, scrs=None):
                wk, t_wk = acquire("w_in", 0, KC, OFF_K, OFF_K + 256)
                halo_m = tiles_mode == "halo"
                kunits = [(kvh, -1) for kvh in range(4)] if halo_m else [(kvh, tg) for kvh in range(4) for tg in range(NG)]
                kst = {}

                def geom(tg):
                    if tg < 0:
                        return 128, slice((NT - 1) * 128, NT * 128), [t_UT[NT - 1]], slice(0, 128), slice(0, 128), [t_KT[0]]
                    return (MW, slice(tg * MW, (tg + 1) * MW), tg_tiles(t_UT, tg), slice(tg * MW, (tg + 1) * MW),
                            slice(128 + tg * MW, 128 + (tg + 1) * MW), t_KT[1 + tg * TPG:1 + (tg + 1) * TPG])

                def k_head(u):
                    kvh, tg = kunits[u]
                    n, ucols, tu, tcols, kcols, tk = geom(tg)
                    pb, tb = bank("acc")
                    for k in range(KC):
                        for half in range(2):
                            P.I("pe", "matmul", [t_wk] + tu, [tb], inc=(k == KC - 1 and half == 1),
                                out=pb[half * 64:(half + 1) * 64, 0:n], lhsT=wk[:, k, kvh * 64:(kvh + 1) * 64],
                                rhs=UT[:, k, ucols], start=(k == 0), stop=(k == KC - 1))
                    sc_, tsc_ = scrs[u % 2]
                    P.I("act", "activation", [tb], [tsc_], out=sc_[0][:, 0:n], in_=pb[:, 0:n], func=AF.Copy)
                    kst[u] = (pb, tb)

                def k_tail(u):
                    kvh, tg = kunits[u]
                    n, ucols, tu, tcols, kcols, tk = geom(tg)
                    pb, tb = kst.pop(u)
                    (qb, r1, r2), tsc_ = scrs[u % 2]
                    pb2, tb2 = bank("aux")
                    P.I("pe", "matmul", [tsc_, t_cb], [tb2], out=pb2[:, 0:n], lhsT=rotmb, rhs=qb[:, 0:n], start=True, stop=True)
                    P.I("dve", "tensor_tensor", [tb, t_trig], [tsc_], out=r1[:, 0:n], in0=pb[:, 0:n], in1=cosT[:, tcols], op=ALU.mult)
                    P.I("dve", "tensor_tensor", [tb2, t_trig], [tsc_], out=r2[:, 0:n], in0=pb2[:, 0:n], in1=sinT[:, tcols], op=ALU.mult)
                    P.I("dve", "tensor_tensor", [tsc_], tk, out=KTd[:, kvh, kcols], in0=r1[:, 0:n], in1=r2[:, 0:n], op=ALU.add)

                k_head(0)
                for u in range(len(kunits)):
                    if u + 1 < len(kunits):
                        k_head(u + 1)
                    k_tail(u)

            def v_proj(tiles_mode):
                wv, t_wv = acquire("w_in", 0, KC, OFF_V, OFF_V + 256)
                vt = [NT - 1] if tiles_mode == "halo" else list(range(NT))
                for ti in vt:
                    pb, tb = bank("acc")
                    for k in range(KC):
                        P.I("pe", "matmul", [t_wv, t_UT[ti]], [tb], inc=(k == KC - 1), out=pb[:, 0:256],
                            lhsT=UT[:, k, ti * 128:(ti + 1) * 128], rhs=wv[:, k, :], start=(k == 0), stop=(k == KC - 1))
                    slot = 0 if tiles_mode == "halo" else ti + 1
                    P.I("act", "activation", [tb], [t_V[slot]], out=Vt[:, slot, :], in_=pb[:, 0:256], func=AF.Copy)

            def dt_path(dtv, t_dtv, abf=None, fold_suffix=False):
                wd, t_wd = acquire("w_in", 0, KC, OFF_DT, OFF_DT + 32)
                pb, tb = bank("aux")
                for ti in range(NT):
                    for k in range(KC):
                        P.I("pe", "matmul", [t_wd, t_UT[ti]], [tb], inc=(k == KC - 1), out=pb[:, ti * 32:(ti + 1) * 32],
                            lhsT=UT[:, k, ti * 128:(ti + 1) * 128], rhs=wd[:, k, :], start=(k == 0), stop=(k == KC - 1))
                v = lambda i: dtv[:, i, :, :]
                raw = pb[:, 0:NT * 32].rearrange("p (t h) -> p t h", t=NT)
                P.I("dve", "tensor_tensor", [tb, t_cst], [t_dtv], out=v(7), in0=raw, in1=bc(C("dtb").unsqueeze(1), [128, NT, 32]), op=ALU.add)
                P.I("dve", "tensor_scalar", [t_dtv], [t_dtv], out=v(7), in0=v(7), scalar1=80.0, scalar2=None, op0=ALU.min)
                P.I("act", "activation", [t_dtv], [t_dtv], out=v(7), in_=v(7), func=AF.Exp)
                P.I("act", "activation", [t_dtv], [t_dtv], out=v(0), in_=v(7), func=AF.Ln, bias=C("one"))
                P.I("dve", "tensor_tensor", [t_dtv, t_cb], [t_dtv], out=v(1), in0=v(0), in1=bc(aneg.unsqueeze(1), [128, NT, 32]), op=ALU.mult)
                pb2, tb2 = bank("aux")
                for ti in range(NT):
                    P.I("pe", "matmul", [t_dtv, t_cst], [tb2], inc=False, out=pb2[:, ti * 64:ti * 64 + 32], lhsT=C("tri"), rhs=dtv[:, 1, ti, :], start=True, stop=True)
                    P.I("pe", "matmul", [t_dtv, t_cst], [tb2], inc=(ti == NT - 1), out=pb2[:, ti * 64 + 32:ti * 64 + 64], lhsT=C("ones"), rhs=dtv[:, 1, ti, :], start=True, stop=True)
                cc = pb2[:, 0:NT * 64].rearrange("p (t c) -> p t c", t=NT)
                P.I("act", "activation", [tb2], [t_dtv], out=v(2), in_=cc[:, :, 0:32], func=AF.Copy)
                P.I("act", "activation", [tb2], [t_dtv], out=v(3), in_=cc[:, :, 32:64], func=AF.Copy)
                P.I("dve", "tensor_tensor", [t_dtv], [t_dtv], out=v(7), in0=v(3), in1=v(2), op=ALU.subtract)
                P.I("act", "activation", [t_dtv], [t_dtv], out=v(4), in_=v(7), func=AF.Exp)
                P.I("act", "activation", [t_dtv], [t_dtv], out=v(5), in_=v(3), func=AF.Exp)
                P.I("dve", "tensor_tensor", [t_dtv], [t_dtv], out=v(6), in0=v(0), in1=v(4), op=ALU.mult)
                if fold_suffix:
                    P.I("dve", "memset", [], [t_dtv], ap=dtv[:, 7, NT - 1, :], constant=0.0)
                    for c in range(NT - 2, -1, -1):
                        P.I("dve", "tensor_tensor", [t_dtv], [t_dtv], out=dtv[:, 7, c, :], in0=dtv[:, 7, c + 1, :], in1=dtv[:, 3, c + 1, :], op=ALU.add)
                    P.I("act", "activation", [t_dtv], [t_dtv], out=v(7), in_=v(7), func=AF.Exp)
                    P.I("dve", "tensor_tensor", [t_dtv], [t_dtv], out=v(6), in0=v(6), in1=v(7), op=ALU.mult)
                if abf is not None:
                    P.I("dve", "tensor_copy", [t_dtv], [t_dtv], out=abf, in_=v(1))
                    P.I("dve", "tensor_tensor", [t_dtv], [t_dtv], out=v(2), in0=v(1), in1=abf, op=ALU.subtract)

            def xbc_path(XBB, t_XBB, full):
                m = A.mark()
                pre = [A.alloc([128, 4 + MW], F32) for _ in range(2)]
                t_pre = P.tiles("pre", 2)
                acc = [A.alloc([128, MW], F32) for _ in range(2)]
                t_accb = P.tiles("cacc", 2)
                units = []
                for pc in range(12):
                    for cb in range(2):
                        if pc >= 10 and not full:
                            units.append((pc, cb, -1))
                        else:
                            for tg in range(NG):
                                units.append((pc, cb, tg))
                stt = {"n": 0}

                def head(u):
                    pc, cb, tg = units[u]
                    blk = pc * 2 + cb
                    if cb == 0 and tg <= 0:
                        stt["w"] = acquire("w_in", 0, KC, OFF_XBC + pc * 256, OFF_XBC + (pc + 1) * 256)
                    wp, t_wp = stt["w"]
                    if tg < 0:
                        pb, tb = bank("acc")
                        for k in range(KC):
                            P.I("pe", "matmul", [t_wp, t_UT[NT - 1]], [tb], inc=(k == KC - 1), out=pb[:, 0:128],
                                lhsT=wp[:, k, cb * 128:(cb + 1) * 128], rhs=UT[:, k, (NT - 1) * 128:NT * 128], start=(k == 0), stop=(k == KC - 1))
                        P.I("act", "activation", [tb], [t_halo[blk]], out=halo[:, blk, 0:3], in_=pb[:, 125:128], func=AF.Copy)
                        return
                    pb, tb = bank("acc")
                    proj_ws(wp, t_wp, cb, KC, UT, tg_tiles(t_UT, tg), tg, pb, tb)
                    s = stt["n"] % 2
                    stt["n"] += 1
                    stt[u] = s
                    P.I("pool", "tensor_copy", [t_halo[blk]], [t_pre[s]], out=pre[s][:, 0:3], in_=halo[:, blk, 0:3])
                    P.I("act", "activation", [tb], [t_pre[s]], out=pre[s][:, 3:3 + MW], in_=pb[:, 0:MW], func=AF.Copy)
                    P.I("pool", "tensor_copy", [t_pre[s]], [t_halo[blk]], out=halo[:, blk, 0:3], in_=pre[s][:, MW:MW + 3])

                def tail(u):
                    pc, cb, tg = units[u]
                    if tg < 0:
                        return
                    blk = pc * 2 + cb
                    s = stt.pop(u)
                    cw = lambda j: C("convw", blk * 4 + j, blk * 4 + j + 1)
                    P.I("dve", "tensor_scalar", [t_pre[s], t_cst], [t_accb[s]], out=acc[s], in0=pre[s][:, 0:MW], scalar1=cw(0), scalar2=None, op0=ALU.mult)
                    for j in range(1, 4):
                        P.I("dve", "scalar_tensor_tensor", [t_pre[s], t_cst, t_accb[s]], [t_accb[s]], out=acc[s], in0=pre[s][:, j:j + MW],
                            scalar=cw(j), in1=acc[s], op0=ALU.mult, op1=ALU.add)
                    if blk < 16:
                        dst, tdst = XBX[:, blk, tg * MW:(tg + 1) * MW], t_XBX[blk][tg]
                    else:
                        dst, tdst = XBB[:, blk - 16, tg * MW:(tg + 1) * MW], t_XBB[blk - 16][tg]
                    P.I("act", "activation", [t_accb[s], t_cst], [tdst], out=dst, in_=acc[s], func=AF.Silu, bias=C("convb", blk, blk + 1))

                head(0)
                for u in range(len(units)):
                    if u + 1 < len(units):
                        head(u + 1)
                    tail(u)
                A.reset(m)
                P.barrier()

            def ssd_chunks(XBB, t_XBB, dtv, t_dtv, is_prefix, abf=None):
                m = A.mark()
                xd = A.alloc([128, D], BF16)
                xdw = A.alloc([128, D], BF16)
                t_xd = P.tile("xd")
                t_xdw = P.tile("xdw")
                Btok = A.alloc([128, 4, 128], BF16)
                t_Btok = P.tile("Btok")
                if not is_prefix:
                    cbm = A.alloc([128, 4, 128], BF16)
                    t_cbm = P.tile("cbm")
                    ypre = [A.alloc([128, 4, 128], F32) for _ in range(2)]
                    t_ypre = P.tiles("ypre", 2)
                    NAT = 4
                    ATh = [A.alloc([128, 4, 128], BF16) for _ in range(NAT)]
                    ATl = [A.alloc([128, 4, 128], BF16) for _ in range(NAT)]
                    t_AT = P.tiles("AT", NAT)
                    eD = [A.alloc([128, 4, 128], BF16) for _ in range(2)]
                    eE = [A.alloc([128, 4, 128], BF16) for _ in range(2)]
                    t_eD = P.tiles("eD", 2)
                    t_eE = P.tiles("eE", 2)
                    MT = [A.alloc([128, 4, 128], BF16) for _ in range(2)]
                    CsT = [A.alloc([128, 4, 128], BF16) for _ in range(2)]
                    t_MT = P.tiles("MT", 2)
                    t_CsT = P.tiles("CsT", 2)
                if is_prefix:
                    sbanks = [bank("acc") for _ in range(4)]
                for c in range(NT):
                    tg = c // TPG
                    cs = slice(c * 128, (c + 1) * 128)
                    for hb in range(2):
                        pb, tb = bank("aux")
                        pv = pb[:].bitcast(BF16)
                        for k in range(8):
                            fb = hb * 8 + k
                            P.I("pe", "transpose", [t_XBX[fb][tg], t_cb], [tb], inc=(k == 7), out=pv[:, k * 128:(k + 1) * 128],
                                in_=XBX[:, fb, cs], identity=identb)
                        src = pv.rearrange("p (h d) -> p h d", h=16)
                        hs = slice(hb * 16, (hb + 1) * 16)
                        if not is_prefix:
                            P.I("dve", "tensor_tensor", [tb, t_dtv], [t_xd], out=xd[:, hb * 1024:(hb + 1) * 1024].rearrange("p (h d) -> p h d", h=16),
                                in0=src, in1=bc(dtv[:, 0, c, hs].unsqueeze(2), [128, 16, 64]), op=ALU.mult)
                        P.I("dve", "tensor_tensor", [tb, t_dtv], [t_xdw], out=xdw[:, hb * 1024:(hb + 1) * 1024].rearrange("p (h d) -> p h d", h=16),
                            in0=src, in1=bc(dtv[:, 6, c, hs].unsqueeze(2), [128, 16, 64]), op=ALU.mult)
                    pb, tb = bank("aux")
                    pv = pb[:].bitcast(BF16)
                    for g in range(4):
                        P.I("pe", "transpose", [t_XBB[g][tg], t_cb], [tb], inc=(g == 3), out=pv[:, g * 128:(g + 1) * 128],
                            in_=XBB[:, g, cs], identity=identb)
                    P.I("act", "activation", [tb], [t_Btok], out=Btok, in_=pv[:, 0:512].rearrange("p (g n) -> p g n", g=4), func=AF.Copy)
                    if not is_prefix:
                        pb, tb = bank("aux")
                        for g in range(4):
                            P.I("pe", "matmul", [t_XBB[g][tg], t_XBB[4 + g][tg]], [tb], inc=(g == 3), out=pb[:, g * 128:(g + 1) * 128],
                                lhsT=XBB[:, g, cs], rhs=XBB[:, 4 + g, cs], start=True, stop=True)
                        P.I("dve", "tensor_tensor", [tb, t_cst], [t_cbm], out=cbm, in0=pb[:, 0:512].rearrange("p (g l) -> p g l", g=4),
                            in1=bc(C("tri").unsqueeze(1), [128, 4, 128]), op=ALU.mult)
                        ybanks = [bank("acc") for _ in range(4)]

                        def qX(hq):
                            g = hq // 2
                            s = hq % 2
                            sa = (c * 8 + hq) % NAT
                            trb = bc(trib.unsqueeze(1), [128, 4, 128])
                            P.I("pool", "tensor_tensor", [t_dtv, t_cb], [t_AT[sa]], out=ATh[sa], in0=bc(abf[:, c, hq * 4:(hq + 1) * 4].unsqueeze(2), [128, 4, 128]), in1=trb, op=ALU.mult)
                            P.I("pool", "tensor_tensor", [t_dtv, t_cb], [t_AT[sa]], out=ATl[sa], in0=bc(dtv[:, 2, c, hq * 4:(hq + 1) * 4].unsqueeze(2), [128, 4, 128]), in1=trb, op=ALU.mult)
                            pbD, tbD = bank("aux")
                            P.I("pe", "matmul", [t_AT[sa], t_cb], [tbD], inc=False, out=pbD[:, 0:512], lhsT=ustrb, rhs=ATh[sa].rearrange("p h l -> p (h l)"), start=True, stop=False)
                            P.I("pe", "matmul", [t_AT[sa], t_cb], [tbD], out=pbD[:, 0:512], lhsT=ustrb, rhs=ATl[sa].rearrange("p h l -> p (h l)"), start=False, stop=True)
                            pbE, tbE = bank("aux")
                            P.I("pe", "matmul", [t_AT[sa], t_cb], [tbE], inc=False, out=pbE[:, 0:512], lhsT=onesb, rhs=ATh[sa].rearrange("p h l -> p (h l)"), start=True, stop=False)
                            P.I("pe", "matmul", [t_AT[sa], t_cb], [tbE], out=pbE[:, 0:512], lhsT=onesb, rhs=ATl[sa].rearrange("p h l -> p (h l)"), start=False, stop=True)
                            P.I("act", "activation", [tbD], [t_eD[s]], out=eD[s].rearrange("p h l -> p (h l)"), in_=pbD[:, 0:512], func=AF.Exp)
                            P.I("act", "activation", [tbE], [t_eE[s]], out=eE[s].rearrange("p h l -> p (h l)"), in_=pbE[:, 0:512], func=AF.Exp)
                            P.I("dve", "tensor_tensor", [t_eD[s], t_cbm], [t_MT[s]], out=MT[s], in0=eD[s], in1=bc(cbm[:, g, :].unsqueeze(1), [128, 4, 128]), op=ALU.mult)
                            P.I("dve", "tensor_tensor", [t_eE[s], t_XBB[4 + g][tg]], [t_CsT[s]], out=CsT[s], in0=eE[s],
                                in1=bc(XBB[:, 4 + g, cs].unsqueeze(1), [128, 4, 128]), op=ALU.mult)

                        def qY(hq):
                            g = hq // 2
                            s = hq % 2
                            for hl in range(4):
                                h = hq * 4 + hl
                                pair = h // 2
                                pby, tby = ybanks[pair // 4]
                                o = pby[(h % 2) * 64:(h % 2 + 1) * 64, (pair % 4) * 128:(pair % 4 + 1) * 128]
                                P.I("pe", "matmul", [t_xd, t_MT[s]], [tby], inc=False, out=o, lhsT=xd[:, h * 64:(h + 1) * 64], rhs=MT[s][:, hl, :], start=True, stop=False)
                                P.I("pe", "matmul", [t_Sbf[g], t_CsT[s]], [tby], inc=(h % 8 == 7), out=o, lhsT=Sbf[:, g, (h % 8) * 64:(h % 8 + 1) * 64],
                                    rhs=CsT[s][:, hl, :], start=False, stop=True)

                        qX(0)
                        for hq in range(8):
                            if hq + 1 < 8:
                                qX(hq + 1)
                            qY(hq)
                        for b4 in range(4):
                            pby, tby = ybanks[b4]
                            ys = b4 % 2
                            xv = XBX[:, b4 * 4:(b4 + 1) * 4, cs]
                            xt = [t_XBX[fb][tg] for fb in range(b4 * 4, b4 * 4 + 4)]
                            P.I("pool", "tensor_tensor", xt + [t_cst], [t_ypre[ys]], out=ypre[ys], in0=xv,
                                in1=bc(C("dcol", b4 * 4, b4 * 4 + 4).unsqueeze(2), [128, 4, 128]), op=ALU.mult)
                            P.I("dve", "tensor_tensor", [t_ypre[ys], tby], xt, out=xv, in0=ypre[ys], in1=pby[:, 0:512].rearrange("p (a l) -> p a l", a=4), op=ALU.add)
                    if is_prefix:
                        for g in range(4):
                            pb, tb = sbanks[g]
                            P.I("pe", "matmul", [t_Btok, t_xdw], [tb], inc=(c == NT - 1), out=pb[:, 0:512], lhsT=Btok[:, g, :], rhs=xdw[:, g * 512:(g + 1) * 512],
                                start=(c == 0), stop=(c == NT - 1))
                        continue
                    for g in range(4):
                        pb, tb = bank("aux")
                        P.I("pe", "matmul", [t_Btok, t_xdw], [tb], out=pb[:, 0:512], lhsT=Btok[:, g, :], rhs=xdw[:, g * 512:(g + 1) * 512], start=True, stop=True)
                        sv = S32[:, g, :].rearrange("p (h d) -> p h d", h=8)
                        P.I("pool", "tensor_tensor", [t_S[g], t_dtv], [t_S[g]], out=sv, in0=sv, in1=bc(dtv[:, 5, c, g * 8:(g + 1) * 8].unsqueeze(2), [128, 8, 64]), op=ALU.mult)
                        P.I("dve", "tensor_tensor", [t_S[g], tb], [t_S[g]], out=S32[:, g, :], in0=S32[:, g, :], in1=pb[:, 0:512], op=ALU.add)
                        if not is_prefix:
                            if c < NT - 1:
                                P.I("act", "activation", [t_S[g]], [t_Sbf[g]], out=Sbf[:, g, :], in_=S32[:, g, :], func=AF.Copy)
                if is_prefix:
                    for g in range(4):
                        pb, tb = sbanks[g]
                        P.I("dve", "tensor_scalar", [tb, t_cst], [t_S[g]], out=S32[:, g, :], in0=pb[:, 0:512], scalar1=C("flag"), scalar2=None, op0=ALU.mult)
                        P.I("act", "activation", [t_S[g]], [t_Sbf[g]], out=Sbf[:, g, :], in_=S32[:, g, :], func=AF.Copy)
                A.reset(m)

            scr = (A.alloc([128, MW], BF16), A.alloc([128, MW], F32), A.alloc([128, MW], F32))
            t_scr = P.tile("scr")
            m1 = A.mark()
            cosH = A.alloc([128, 128], F32)
            _cosh_off = A.last
            sinH = A.alloc([128, 128], F32)
            _sinh_off = A.last
            t_trigH = P.tile("trigH")
            trig_tables(0, 128, cosH, sinH, t_trigH)
            P.barrier()
            norm_from_dram(xp_d, "gmix")
            stop_here("norm1")
            scrh = (A.alloc([128, 128], BF16), A.alloc([128, 128], F32), A.alloc([128, 128], F32))
            t_scrh = P.tile("scrh")
            k_proj("halo", cosH, sinH, t_trigH, scr, t_scr, [(scr, t_scr), (scrh, t_scrh)])
            if stop_after == "khalo":
                P.barrier()
                P.emit()
                return nc, {"ktd": _ktd_off, "vt": _vt_off, "cosh": _cosh_off, "sinh": _sinh_off}
            v_proj("halo")
            stop_here("kvhalo")
            A.reset(m1)
            XBB = A.alloc([128, 8, TK], BF16)
            t_XBB = [[P.tile("xbb%d_%d" % (b, g)) for g in range(NG)] for b in range(8)]
            dtv = A.alloc([128, 8, NT, 32], F32)
            t_dtv = P.tile("dtv")
            P.barrier()
            dt_path(dtv, t_dtv, fold_suffix=True)
            stop_here("dt1")
            xbc_path(XBB, t_XBB, False)
            stop_here("xbc1")
            ssd_chunks(XBB, t_XBB, dtv, t_dtv, True)
            stop_here("ssd1")
            A.reset_hi(HI_KV)
            A.reset_lo(LO_A)
            P.barrier()
            if stop_after == "prefix":
                P.emit()
                return nc, {"ktd": _ktd_off, "vt": _vt_off}

            scr = (A.alloc([128, MW], BF16), A.alloc([128, MW], F32), A.alloc([128, MW], F32))
            t_scr = P.tile("scr2")
            scr2 = (A.alloc([128, MW], BF16), A.alloc([128, MW], F32), A.alloc([128, MW], F32))
            t_scr2 = P.tile("scr3")
            scrs = [(scr, t_scr), (scr2, t_scr2)]
            QT = A.alloc([128, 8, TK], BF16)
            t_QT = P.tiles("QT", NT)
            norm_from_dram(x_d, "gmix")
            stop_here("norm2")
            m1 = A.mark()
            cosT = A.alloc([128, TK], F32)
            sinT = A.alloc([128, TK], F32)
            t_trig = P.tile("trig")
            trig_tables(128, TK, cosT, sinT, t_trig)
            k_proj("main", cosT, sinT, t_trig, scr, t_scr, scrs)
            v_proj("main")
            stop_here("kvmain")
            qunits = [(pc, cb, tg) for pc in range(4) for cb in range(2) for tg in range(NG)]
            qst = {}

            def q_head(u):
                pc, cb, tg = qunits[u]
                if (cb, tg) == (0, 0):
                    qst["w"] = acquire("w_in", 0, KC, OFF_Q + pc * 256, OFF_Q + (pc + 1) * 256)
                wq, t_wq = qst["w"]
                pb, tb = bank("acc")
                proj_ws(wq, t_wq, cb, KC, UT, tg_tiles(t_UT, tg), tg, pb, tb)
                sc_, tsc_ = scrs[u % 2]
                P.I("act", "activation", [tb], [tsc_], out=sc_[0][:, 0:MW], in_=pb[:, 0:MW], func=AF.Copy)
                qst[u] = (pb, tb)

            def q_tail(u):
                pc, cb, tg = qunits[u]
                hp = pc * 2 + cb
                pb, tb = qst.pop(u)
                (qb, r1, r2), tsc_ = scrs[u % 2]
                pb2, tb2 = bank("aux")
                cosv, sinv = cosT[:, tg * MW:(tg + 1) * MW], sinT[:, tg * MW:(tg + 1) * MW]
                P.I("pe", "matmul", [tsc_, t_cb], [tb2], out=pb2[:, 0:MW], lhsT=rotmb, rhs=qb[:, 0:MW], start=True, stop=True)
                P.I("dve", "tensor_tensor", [tb, t_trig], [tsc_], out=r1[:, 0:MW], in0=pb[:, 0:MW], in1=cosv, op=ALU.mult)
                P.I("dve", "tensor_tensor", [tb2, t_trig], [tsc_], out=r2[:, 0:MW], in0=pb2[:, 0:MW], in1=sinv, op=ALU.mult)
                P.I("dve", "tensor_tensor", [tsc_], tg_tiles(t_QT, tg), out=QT[:, hp, tg * MW:(tg + 1) * MW], in0=r1[:, 0:MW], in1=r2[:, 0:MW], op=ALU.add)

            q_head(0)
            for u in range(len(qunits)):
                if u + 1 < len(qunits):
                    q_head(u + 1)
                q_tail(u)
            A.reset(m1)
            P.barrier()

            stop_here("qkv")
            sc = [A.alloc([128, 16, 256], F32) for _ in range(2)]
            t_sc = P.tiles("sc", 2)
            Pn = A.alloc([128, 16, 256], BF16)
            t_Pn = P.tile("Pn")
            PT = A.alloc([128, 32, 128], BF16)
            t_PT = P.tiles("PT", 4)
            ast = [A.alloc([128, 6, 16], F32) for _ in range(2)]
            t_ast = P.tiles("ast", 2)
            def att_A(j):
                s = j % 2
                mk = C("amask0") if j == 0 else C("amask")
                kcols = slice(j * 128, j * 128 + 256)
                for quad in range(4):
                    bk = [bank("aux"), bank("aux")]
                    for i4 in range(4):
                        h = quad * 4 + i4
                        hh = h % 2
                        hpair = h // 2
                        pb, tb = bk[hh]
                        co = (i4 // 2) * 256
                        P.I("pe", "matmul", [t_QT[j], t_KT[j], t_KT[j + 1]], [tb], inc=(i4 >= 2), out=pb[:, co:co + 256],
                            lhsT=QT[hh * 64:(hh + 1) * 64, hpair, j * 128:(j + 1) * 128], rhs=KTd[hh * 64:(hh + 1) * 64, h // 4, kcols], start=True, stop=True)
                    for hh in range(2):
                        pb, tb = bk[hh]
                        P.I("dve", "tensor_tensor", [tb, t_cst], [t_sc[s]], out=sc[s][:, quad * 4 + hh:quad * 4 + hh + 3:2, :],
                            in0=pb[:, 0:512].rearrange("p (h k) -> p h k", h=2), in1=bc(mk.unsqueeze(1), [128, 2, 256]), op=ALU.add)
                a_ = ast[s]
                ta = t_ast[s]
                P.I("dve", "tensor_reduce", [t_sc[s]], [ta], out=a_[:, 0, :], in_=sc[s], axis=AX.X, op=ALU.max)
                P.I("dve", "scalar_tensor_tensor", [ta, t_cb], [ta], out=a_[:, 1, :], in0=a_[:, 0, :], scalar=-0.125, in1=nsink, op0=ALU.mult, op1=ALU.min)
                P.I("dve", "memset", [], [ta], ap=a_[:, 2, :], constant=0.0)
                P.I("dve", "tensor_tensor", [ta, t_cst], [ta], out=a_[:, 3, :], in0=a_[:, 1, :], in1=C("sinks"), op=ALU.add)

            def att_B(j):
                s = j % 2
                a_ = ast[s]
                ta = t_ast[s]
                for h in range(16):
                    P.I("act", "activation", [t_sc[s], ta], [t_sc[s], ta], out=sc[s][:, h, :], in_=sc[s][:, h, :], func=AF.Exp, scale=0.125,
                        bias=a_[:, 1, h:h + 1], accum_out=a_[:, 2, h:h + 1])
                P.I("act", "activation", [ta], [ta], out=a_[:, 3, :], in_=a_[:, 3, :], func=AF.Exp)

            def att_C1(j):
                s = j % 2
                a_ = ast[s]
                ta = t_ast[s]
                P.I("dve", "tensor_tensor", [ta], [ta], out=a_[:, 4, :], in0=a_[:, 2, :], in1=a_[:, 3, :], op=ALU.add)
                P.I("dve", "reciprocal", [ta], [ta], out=a_[:, 5, :], in_=a_[:, 4, :])
                P.I("dve", "tensor_tensor", [t_sc[s], ta], [t_Pn], out=Pn, in0=sc[s], in1=bc(a_[:, 5, :].unsqueeze(2), [128, 16, 256]), op=ALU.mult)

            def att_C2(j):
                for q4 in range(4):
                    pb, tb = bank("aux")
                    pv = pb[:].bitcast(BF16)
                    for i in range(8):
                        idx = q4 * 8 + i
                        h, kb = idx // 2, idx % 2
                        P.I("pe", "transpose", [t_Pn, t_cb], [tb], inc=(i == 7), out=pv[:, i * 128:(i + 1) * 128], in_=Pn[:, h, kb * 128:(kb + 1) * 128], identity=identb)
                    P.I("act", "activation", [tb], [t_PT[q4]], out=PT[:, q4 * 8:(q4 + 1) * 8, :], in_=pv.rearrange("p (i q) -> p i q", i=8), func=AF.Copy)
                for half in range(2):
                    pb, tb = bank("acc")
                    for pr in range(4):
                        hp = half * 4 + pr
                        for hh in range(2):
                            h = hp * 2 + hh
                            kvh = h // 4
                            for kb in range(2):
                                P.I("pe", "matmul", [t_V[j + kb], t_PT[h // 4]], [tb], inc=(pr == 3 and hh == 1 and kb == 1),
                                    out=pb[hh * 64:(hh + 1) * 64, pr * 128:(pr + 1) * 128], lhsT=Vt[:, j + kb, kvh * 64:(kvh + 1) * 64],
                                    rhs=PT[:, h * 2 + kb, :], start=(kb == 0), stop=(kb == 1))
                    P.I("act", "activation", [tb], [t_attnT[j]], out=attnT[:, half * 4:(half + 1) * 4, j * 128:(j + 1) * 128],
                        in_=pb[:, 0:512].rearrange("p (a q) -> p a q", a=4), func=AF.Copy)

            att_A(0)
            att_B(0)
            if NT > 1:
                att_A(1)
            for j in range(NT):
                att_C1(j)
                if j + 2 < NT:
                    att_A(j + 2)
                if j + 1 < NT:
                    att_B(j + 1)
                att_C2(j)
            A.reset_hi(HI_S)
            P.barrier()
            stop_here("attn")

            XBX = A.lo_alloc([128, 16, TK], BF16)
            t_XBX = [[P.tile("xbxm%d_%d" % (b, g)) for g in range(NG)] for b in range(16)]
            rstd = A.lo_alloc([128, TK], F32)
            t_rstd = P.tiles("rstd", NG)
            XBB = A.alloc([128, 8, TK], BF16)
            t_XBB = [[P.tile("xbbm%d_%d" % (b, g)) for g in range(NG)] for b in range(8)]
            dtv = A.alloc([128, 8, NT, 32], F32)
            abf = A.alloc([128, NT, 32], BF16)
            t_dtv = P.tile("dtvm")
            dt_path(dtv, t_dtv, abf)
            xbc_path(XBB, t_XBB, True)
            stop_here("xbc2")
            ssd_chunks(XBB, t_XBB, dtv, t_dtv, False, abf)
            stop_here("ssd2")
            A.reset_hi(BASE)
            P.barrier()

            szb = [A.alloc([128, MW], F32) for _ in range(2)]
            vb = [A.alloc([128, MW], F32) for _ in range(2)]
            sqb = [A.alloc([128, MW], BF16) for _ in range(2)]
            t_szb = P.tiles("sz", 2)
            t_vb = P.tiles("vb", 2)
            t_sqb = P.tiles("sq", 2)
            ssq_banks = [bank("aux") for _ in range(NG)]
            units = [(pc, cb, tg) for pc in range(8) for cb in range(2) for tg in range(NG)]
            zw = {}

            def z_proj(n):
                pc, cb, tg = units[n]
                if (cb, tg) == (0, 0):
                    zw["w"] = acquire("w_in", 0, KC, OFF_Z + pc * 256, OFF_Z + (pc + 1) * 256)
                wz, t_wz = zw["w"]
                pb, tb = bank("acc")
                proj_ws(wz, t_wz, cb, KC, UT, tg_tiles(t_UT, tg), tg, pb, tb)
                zw[n] = (pb, tb)

            def z_silu(n):
                pb, tb = zw.pop(n)
                s = n % 2
                P.I("act", "activation", [tb], [t_szb[s]], out=szb[s], in_=pb[:, 0:MW], func=AF.Silu)

            def z_tail(n):
                pc, cb, tg = units[n]
                fb = pc * 2 + cb
                s = n % 2
                yv = XBX[:, fb, tg * MW:(tg + 1) * MW]
                P.I("dve", "tensor_tensor", [t_XBX[fb][tg], t_szb[s]], [t_vb[s]], out=vb[s], in0=yv, in1=szb[s], op=ALU.mult)
                P.I("act", "activation", [t_vb[s]], [t_sqb[s]], out=sqb[s], in_=vb[s], func=AF.Square)
                P.I("act", "activation", [t_vb[s], t_cst], [t_XBX[fb][tg]], out=yv, in_=vb[s], func=AF.Copy, scale=C("gssd", fb, fb + 1))
                pbs, tbs = ssq_banks[tg]
                P.I("pe", "matmul", [t_sqb[s], t_cb], [tbs], inc=(fb == 15), out=pbs[:, 0:MW], lhsT=onesb, rhs=sqb[s], start=(fb == 0), stop=(fb == 15))

            z_proj(0)
            z_silu(0)
            for n in range(len(units)):
                if n + 1 < len(units):
                    z_proj(n + 1)
                z_tail(n)
                if n + 1 < len(units):
                    z_silu(n + 1)
            for tg in range(NG):
                pbs, tbs = ssq_banks[tg]
                rv = rstd[:, tg * MW:(tg + 1) * MW]
                P.I("act", "activation", [tbs, t_cst], [t_rstd[tg]], out=rv, in_=pbs[:, 0:MW], func=AF.Sqrt, scale=1.0 / D, bias=C("epss"))
                P.I("dve", "reciprocal", [t_rstd[tg]], [t_rstd[tg]], out=rv, in_=rv)
            stop_here("zgate")
            A.reset_hi(BASE)
            P.barrier()

            MG = A.alloc([128, KC, TK], BF16)
            t_MG = P.tiles("MG", NT)
            HI_MG = A.mark()
            sg = [[A.alloc([128, MW], F32) for _ in range(NG)] for _ in range(2)]
            t_sg = [P.tiles("sg%d_" % c, NG) for c in range(2)]
            t1 = [[A.alloc([128, MW], F32) for _ in range(NG)] for _ in range(2)]
            t_t1 = [P.tiles("t1%d_" % c, NG) for c in range(2)]
            t2 = [A.alloc([128, MW], F32) for _ in range(2)]
            t_t2 = P.tiles("t2", 2)
            n = 0
            for pc in range(8):
                c0, c1 = pc * 256, (pc + 1) * 256
                wga, t_wga = acquire("w_in", 0, KC, OFF_GA + c0, OFF_GA + c1)
                for cb in range(2):
                    for tg in range(NG):
                        pb, tb = bank("acc")
                        proj_ws(wga, t_wga, cb, KC, UT, tg_tiles(t_UT, tg), tg, pb, tb)
                        P.I("act", "activation", [tb], [t_sg[cb][tg]], out=sg[cb][tg], in_=pb[:, 0:MW], func=AF.Sigmoid)
                wab, t_wab = acquire("w_attn_br", 0, 8, c0, c1)
                for cb in range(2):
                    for tg in range(NG):
                        pb, tb = bank("acc")
                        proj_ws(wab, t_wab, cb, 8, attnT, tg_tiles(t_attnT, tg), tg, pb, tb)
                        P.I("dve", "tensor_tensor", [tb, t_sg[cb][tg]], [t_t1[cb][tg]], out=t1[cb][tg], in0=pb[:, 0:MW], in1=sg[cb][tg], op=ALU.mult)
                wgs, t_wgs = acquire("w_in", 0, KC, OFF_GS + c0, OFF_GS + c1)
                for cb in range(2):
                    for tg in range(NG):
                        pb, tb = bank("acc")
                        proj_ws(wgs, t_wgs, cb, KC, UT, tg_tiles(t_UT, tg), tg, pb, tb)
                        P.I("act", "activation", [tb], [t_sg[cb][tg]], out=sg[cb][tg], in_=pb[:, 0:MW], func=AF.Sigmoid)
                        P.I("pool", "tensor_tensor", [t_sg[cb][tg], t_rstd[tg]], [t_sg[cb][tg]], out=sg[cb][tg], in0=sg[cb][tg], in1=rstd[:, tg * MW:(tg + 1) * MW], op=ALU.mult)
                wsb, t_wsb = acquire("w_ssd_br", 0, KC, c0, c1)
                for cb in range(2):
                    fb = pc * 2 + cb
                    for tg in range(NG):
                        pb, tb = bank("acc")
                        proj_ws(wsb, t_wsb, cb, KC, XBX, [t_XBX[k][tg] for k in range(16)], tg, pb, tb)
                        s = n % 2
                        n += 1
                        P.I("dve", "tensor_tensor", [tb, t_sg[cb][tg]], [t_t2[s]], out=t2[s], in0=pb[:, 0:MW], in1=sg[cb][tg], op=ALU.mult)
                        P.I("dve", "tensor_tensor", [t_t2[s], t_t1[cb][tg]], tg_tiles(t_MG, tg), out=MG[:, fb, tg * MW:(tg + 1) * MW], in0=t2[s], in1=t1[cb][tg], op=ALU.add)
            A.reset_lo(BASE)
            P.barrier()
            stop_here("merge")

            H = A.lo_alloc([128, NT, D], F32)
            t_H = P.tiles("H", NT, dma=True)
            for ti in range(NT):
                P.dma("sp", t_H[ti], DR, H[:, ti, :], x_d[ti * 128:(ti + 1) * 128, :])
            A.reset_hi(HI_MG)
            nctx = norm_ctx(3)
            LAG = 2
            for pc in range(8):
                c0, c1 = pc * 256, (pc + 1) * 256
                wo, t_wo = acquire("w_o", 0, KC, c0, c1)
                for ti in range(NT):
                    pb, tb = bank("acc")
                    for k in range(KC):
                        P.I("pe", "matmul", [t_wo, t_MG[ti]], [tb], inc=(k == KC - 1), out=pb[:, 0:256], lhsT=MG[:, k, ti * 128:(ti + 1) * 128],
                            rhs=wo[:, k, :], start=(k == 0), stop=(k == KC - 1))
                    P.I("dve", "tensor_tensor", [tb, t_H[ti]], [t_H[ti]], out=H[:, ti, c0:c1], in0=pb[:, 0:256], in1=H[:, ti, c0:c1], op=ALU.add)
                    if pc == 7:
                        norm_s1(nctx, H[:, ti, :], t_H[ti], ti)
                        if ti >= LAG:
                            norm_s2(nctx, "gffn", ti - LAG)
            for ti in range(max(0, NT - LAG), NT):
                norm_s2(nctx, "gffn", ti)
            A.reset_hi(BASE)
            P.barrier()
            stop_here("wo")

            def norm_from_H(gname):
                m = A.mark()
                ub = [A.alloc([128, D], BF16) for _ in range(2)]
                t_ub = P.tiles("ubh", 2)
                junk = A.alloc([128, D], BF16)
                t_junk = P.tile("junkh")
                norm_stage1(H[:, 0, :], t_H[0], ub[0], t_ub[0], junk, t_junk)
                for ti in range(NT):
                    if ti + 1 < NT:
                        s = (ti + 1) % 2
                        norm_stage1(H[:, ti + 1, :], t_H[ti + 1], ub[s], t_ub[s], junk, t_junk)
                    norm_stage2(gname, ti, ub[ti % 2], t_ub[ti % 2])
                A.reset(m)
                P.barrier()

            ACTT = A.alloc([128, 16, TK], BF16)
            t_ACT = [[P.tile("act%d_%d" % (c, g)) for g in range(NG)] for c in range(16)]
            sgf = [[A.alloc([128, MW], F32) for _ in range(NG)] for _ in range(2)]
            t_sgf = [P.tiles("sgf%d_" % c, NG) for c in range(2)]
            NCH = FF // 128
            pctx = norm_ctx(2)
            LAGP = 1
            groups = []
            c_ = 0
            while c_ < NCH:
                g_ = min(8, NCH - c_)
                groups.append((c_, g_))
                c_ += g_

            def ffn_gateup(gi):
                ch0, gch = groups[gi]
                ring = (gi % 2) * 8
                for pp in range(gch // 2):
                    c0 = (ch0 + pp * 2) * 128
                    wgt, t_wgt = acquire("w_gate", 0, KC, c0, c0 + 256)
                    for cb in range(2):
                        for tg in range(NG):
                            pb, tb = bank("acc")
                            proj_ws(wgt, t_wgt, cb, KC, UT, tg_tiles(t_UT, tg), tg, pb, tb)
                            P.I("act", "activation", [tb], [t_sgf[cb][tg]], out=sgf[cb][tg], in_=pb[:, 0:MW], func=AF.Silu)
                    wup, t_wup = acquire("w_up", 0, KC, c0, c0 + 256)
                    for cb in range(2):
                        slot = ring + pp * 2 + cb
                        for tg in range(NG):
                            pb, tb = bank("acc")
                            proj_ws(wup, t_wup, cb, KC, UT, tg_tiles(t_UT, tg), tg, pb, tb)
                            P.I("dve", "tensor_tensor", [tb, t_sgf[cb][tg]], [t_ACT[slot][tg]], out=ACTT[:, slot, tg * MW:(tg + 1) * MW], in0=pb[:, 0:MW], in1=sgf[cb][tg], op=ALU.mult)

            def ffn_down(gi):
                ch0, gch = groups[gi]
                ring = (gi % 2) * 8
                last_grp = gi == len(groups) - 1
                for cp in range(4):
                    wdn, t_wdn = acquire("w_down", ch0, ch0 + gch, cp * 512, (cp + 1) * 512)
                    for ti in range(NT):
                        tg = ti // TPG
                        pb, tb = bank("acc")
                        for k in range(gch):
                            P.I("pe", "matmul", [t_wdn, t_ACT[ring + k][tg]], [tb], inc=(k == gch - 1), out=pb[:, 0:512], lhsT=ACTT[:, ring + k, ti * 128:(ti + 1) * 128],
                                rhs=wdn[:, k, :], start=(k == 0), stop=(k == gch - 1))
                        P.I("dve", "tensor_tensor", [tb, t_H[ti]], [t_H[ti]], out=H[:, ti, cp * 512:(cp + 1) * 512], in0=pb[:, 0:512], in1=H[:, ti, cp * 512:(cp + 1) * 512], op=ALU.add)
                        if last_grp and cp == 3:
                            norm_s1(pctx, H[:, ti, :], t_H[ti], ti)
                            if ti >= LAGP:
                                norm_s2(pctx, "gple", ti - LAGP)
                if last_grp:
                    for ti in range(max(0, NT - LAGP), NT):
                        norm_s2(pctx, "gple", ti)

            ffn_gateup(0)
            for gi in range(len(groups)):
                if gi + 1 < len(groups):
                    ffn_gateup(gi + 1)
                ffn_down(gi)
            A.reset_hi(BASE)
            P.barrier()
            stop_here("ffn")

            pT = A.alloc([128, 2, TK], BF16)
            t_pT = P.tiles("pT", NT)
            ps = [A.alloc([128, PLE], F32) for _ in range(2)]
            t_ps = P.tiles("ps", 2, dma=True)
            pbf = [A.alloc([128, PLE], BF16) for _ in range(2)]
            t_pbf = P.tiles("pbf", 2)
            for ti in range(NT):
                s = ti % 2
                P.dma("sp", t_ps[s], DR, ps[s], p_d[ti * 128:(ti + 1) * 128, :])
                P.I("dve", "tensor_copy", [t_ps[s]], [t_pbf[s]], out=pbf[s], in_=ps[s])
                pb, tb = bank("aux")
                pv = pb[:].bitcast(BF16)
                for k in range(2):
                    P.I("pe", "transpose", [t_pbf[s], t_cb], [tb], inc=(k == 1), out=pv[:, k * 128:(k + 1) * 128], in_=pbf[s][:, k * 128:(k + 1) * 128], identity=identb)
                P.I("dve", "tensor_copy", [tb], [t_pT[ti]], out=pT[:, :, ti * 128:(ti + 1) * 128], in_=pv[:, 0:256].rearrange("p (k t) -> p k t", k=2))
            sgp = [A.alloc([128, 256], F32) for _ in range(NT)]
            t_sgp = P.tiles("sgp", NT)
            tp = [A.alloc([128, 256], F32) for _ in range(2)]
            t_tp = P.tiles("tp", 2)
            n = 0
            for pc in range(8):
                c0, c1 = pc * 256, (pc + 1) * 256
                wpg, t_wpg = acquire("w_ple_gate", 0, KC, c0, c1)
                for ti in range(NT):
                    pb, tb = bank("acc")
                    for k in range(KC):
                        P.I("pe", "matmul", [t_wpg, t_UT[ti]], [tb], inc=(k == KC - 1), out=pb[:, 0:256], lhsT=UT[:, k, ti * 128:(ti + 1) * 128],
                            rhs=wpg[:, k, :], start=(k == 0), stop=(k == KC - 1))
                    P.I("act", "activation", [tb], [t_sgp[ti]], out=sgp[ti], in_=pb[:, 0:256], func=AF.Sigmoid)
                wpp, t_wpp = acquire("w_ple_proj", 0, 2, c0, c1)
                for ti in range(NT):
                    pb, tb = bank("acc")
                    for k in range(2):
                        P.I("pe", "matmul", [t_wpp, t_pT[ti]], [tb], inc=(k == 1), out=pb[:, 0:256], lhsT=pT[:, k, ti * 128:(ti + 1) * 128],
                            rhs=wpp[:, k, :], start=(k == 0), stop=(k == 1))
                    s = n % 2
                    n += 1
                    P.I("dve", "tensor_tensor", [tb, t_sgp[ti]], [t_tp[s]], out=tp[s], in0=pb[:, 0:256], in1=sgp[ti], op=ALU.mult)
                    P.I("dve", "tensor_tensor", [t_tp[s], t_H[ti]], [t_H[ti]], out=H[:, ti, c0:c1], in0=tp[s], in1=H[:, ti, c0:c1], op=ALU.add)
            A.reset_hi(BASE)
            P.barrier()
            stop_here("ple")

            gfb = A.alloc([128, D], F32)
            t_gfb = P.tile("gfb", dma=True)
            P.dma("sp", t_gfb, DR, gfb, gfin_d.partition_broadcast(128))
            ost = [A.alloc([128, D], F32) for _ in range(2)]
            t_ost = P.tiles("ost", 2, dma=True)
            junk = A.alloc([128, D], BF16)
            t_junk = P.tile("junkf")
            outs = []
            for ti in range(NT):
                s = ti % 2
                ss, t_ss = stat_slot()
                P.I("dve", "memset", [], [t_ss], ap=ss[:, 0:1], constant=0.0)
                P.I("act", "activation", [t_H[ti], t_ss], [t_junk, t_ss], out=junk, in_=H[:, ti, :], func=AF.Square, accum_out=ss[:, 0:1])
                P.I("act", "activation", [t_ss, t_cst], [t_ss], out=ss[:, 1:2], in_=ss[:, 0:1], func=AF.Sqrt, scale=1.0 / D, bias=C("epsn"))
                P.I("dve", "reciprocal", [t_ss], [t_ss], out=ss[:, 2:3], in_=ss[:, 1:2])
                P.I("dve", "scalar_tensor_tensor", [t_H[ti], t_ss, t_gfb], [t_ost[s]], out=ost[s], in0=H[:, ti, :], scalar=ss[:, 2:3], in1=gfb, op0=ALU.mult, op1=ALU.mult)
                outs.append(P.dma("sp", DR, t_ost[s], y_d[ti * 128:(ti + 1) * 128, :], ost[s], sem_t=t_ost[s]))
            for ev in outs[-2:]:
                P.wait_on("sp", ev)

        except _Stop:
            pass
        if plan is not None:
            P.emit()
    build.phases = PH
    return nc, ws["req"]


def _pack_cst(inp, hf, NT):
    c = np.zeros((128, CST_W), np.float32)

    def put(name, arr):
        o, w = CST[name]
        c[:, o:o + w] = arr

    k = np.arange(128)
    put("ident", np.eye(128, dtype=np.float32))
    put("tri", (k[:, None] <= k[None, :]).astype(np.float32))
    put("ustr", (k[:, None] > k[None, :]).astype(np.float32))
    put("ones", np.ones((128, 128), np.float32))
    rot = np.zeros((128, 128), np.float32)
    for pp in range(128):
        d = pp % 64
        if d < 32:
            rot[pp + 32, pp] = -1.0
        else:
            rot[pp - 32, pp] = 1.0
    put("rotm", rot)
    am = np.full((128, 256), MASKV, np.float32)
    q = k[:, None]
    kk = k[None, :]
    am[:, 0:128][kk > q] = 0.0
    am[:, 128:256][kk <= q] = 0.0
    put("amask", am)
    am0 = am.copy()
    if hf == 0:
        am0[:, 0:128] = MASKV
    put("amask0", am0)
    half = 32
    invf = (10000.0 ** (-np.arange(half, dtype=np.float32) * 2.0 / 64)).astype(np.float32)
    put("invf", invf[k % 32][:, None])
    put("flag", np.full((128, 1), float(hf), np.float32))
    put("epsn", np.full((128, 1), EPS, np.float32))
    put("epss", np.full((128, 1), SSM_EPS, np.float32))
    put("one", np.ones((128, 1), np.float32))
    col = lambda v: np.ascontiguousarray(np.asarray(v, np.float32).reshape(-1, 128).T)
    put("gmix", col(inp["g_mix"][0]))
    put("gffn", col(inp["g_ffn"][0]))
    put("gple", col(inp["g_ple"][0]))
    put("gssd", col(inp["g_ssd"][0]))
    cw = np.asarray(inp["conv_w"][0], np.float32)
    put("convw", np.ascontiguousarray(cw.reshape(4, 24, 128).transpose(2, 1, 0)).reshape(128, 96))
    put("convb", col(inp["conv_b"][0]))
    ds = np.asarray(inp["d_skip"][0], np.float32)
    put("dcol", np.ascontiguousarray(np.repeat(ds, 64).reshape(16, 128).T))
    put("dtb", np.broadcast_to(np.asarray(inp["dt_bias"][0], np.float32)[None, :], (128, 32)))
    put("alog", np.broadcast_to(np.asarray(inp["a_log"][0], np.float32)[None, :], (128, 32)))
    put("sinks", np.broadcast_to(np.asarray(inp["sinks"][0], np.float32)[None, :], (128, 16)))
    return c


_CACHE = {}


def _get_nc(NT):
    if NT not in _CACHE:
        _, req = build(NT, plan=None)
        nc, _ = build(NT, plan=req)
        _CACHE[NT] = nc
    return _CACHE[NT]


def make_in_maps(inp, NT):
    x = np.asarray(inp["x"], np.float32)
    B, S, _ = x.shape
    TK = NT * 128
    assert S == 2 * TK
    pos = np.asarray(inp["positions"]).astype(np.int32)
    p = np.asarray(inp["p"], np.float32)[0]
    shared = {k: np.ascontiguousarray(np.asarray(inp[k], np.float32)[0]) for k in WSPEC}
    shared["g_final"] = np.ascontiguousarray(np.asarray(inp["g_final"], np.float32))
    maps = []
    for b in range(B):
        for hf in range(2):
            t0 = hf * TK
            m = dict(shared)
            m["x"] = np.ascontiguousarray(x[b, t0:t0 + TK])
            m["xprev"] = np.ascontiguousarray(x[b, 0:TK]) if hf == 1 else np.zeros((TK, D), np.float32)
            pe = np.zeros(128 + TK, np.int32)
            pe[128:] = pos[b, t0:t0 + TK]
            if hf == 1:
                pe[:128] = pos[b, t0 - 128:t0]
            m["pos"] = pe
            m["p"] = np.ascontiguousarray(p[b, t0:t0 + TK])
            m["cst"] = _pack_cst(inp, hf, NT)
            maps.append(m)
    return maps


def kernel(**inputs):
    x = np.asarray(inputs["x"])
    B, S, _ = x.shape
    NT = S // 256
    nc = _get_nc(NT)
    maps = make_in_maps(inputs, NT)
    res = run_bass_kernel_spmd(nc, maps, core_ids=list(range(len(maps))))
    out = np.zeros((B, S, D), np.float32)
    TK = NT * 128
    i = 0
    for b in range(B):
        for hf in range(2):
            out[b, hf * TK:(hf + 1) * TK] = res.results[i]["y"]
            i += 1
    return out
```
